# Optimizing a Trainium2 kernel written in Bass

```python
import math
import jax
import jax.numpy as jnp
from jax import lax
import numpy as np

D_MODEL = 2048
BATCH = 8
SEQ = 2048
DEPTH = 2

N_MIXERS = 2
RMS_EPS = 1e-6

SSM_EXPAND = 2
SSM_D_INNER = SSM_EXPAND * D_MODEL
SSM_HEAD_DIM = 64
SSM_N_HEADS = SSM_D_INNER // SSM_HEAD_DIM
SSM_D_STATE = 128
SSM_N_GROUPS = 8
SSM_CONV_WIDTH = 4
SSM_CHUNK = 128
SSM_CONV_DIM = SSM_D_INNER + 2 * SSM_N_GROUPS * SSM_D_STATE
SSM_IN_DIM = 2 * SSM_D_INNER + 2 * SSM_N_GROUPS * SSM_D_STATE + SSM_N_HEADS

ATTN_HEAD_DIM = 128
ATTN_GROUP_HEADS = 8
ATTN_PATTERNS = ((128, 1), (512, 4), (2048, 16))
ATTN_N_GROUPS = len(ATTN_PATTERNS)
ATTN_BLOCK = 128
ATTN_QKV_DIM = ATTN_N_GROUPS * 3 * ATTN_GROUP_HEADS * ATTN_HEAD_DIM
ATTN_OUT_DIM = ATTN_GROUP_HEADS * ATTN_HEAD_DIM
ROPE_THETA = 10000.0

D_FF = -(-8 * D_MODEL // (3 * 256)) * 256

N_SSM_LAYERS = (DEPTH + 1) // 2
N_ATTN_LAYERS = DEPTH // 2

kernel_name = "hybrid_ssd_dilated_swa_swiglu"


def rms_norm(x, gain):
    xf = x.astype(jnp.float32)
    y = xf * lax.rsqrt(jnp.mean(xf * xf, axis=-1, keepdims=True) + RMS_EPS)
    return (y * gain.astype(jnp.float32)).astype(x.dtype)


def causal_depthwise_conv(u, w, b):
    k_width = w.shape[0]
    out = lax.conv_general_dilated(
        u, w[:, None, :].astype(u.dtype), window_strides=(1,), padding=[(k_width - 1, 0)],
        dimension_numbers=('NWC', 'WIO', 'NWC'), feature_group_count=u.shape[-1])
    return out + b


def ssd_chunked(x, a, b, c):
    out_dtype = x.dtype
    f32 = jnp.float32
    bsz, seq, n_heads, head_dim = x.shape
    n_groups, d_state = b.shape[2], b.shape[3]
    reps = n_heads // n_groups
    n_chunks = seq // SSM_CHUNK
    x = x.astype(f32).reshape(bsz, n_chunks, SSM_CHUNK, n_groups, reps, head_dim)
    a = a.astype(f32).reshape(bsz, n_chunks, SSM_CHUNK, n_groups, reps).transpose(0, 3, 4, 1, 2)
    b = b.astype(f32).reshape(bsz, n_chunks, SSM_CHUNK, n_groups, d_state)
    c = c.astype(f32).reshape(bsz, n_chunks, SSM_CHUNK, n_groups, d_state)
    a_cum = jnp.cumsum(a, axis=-1)
    causal = jnp.tril(jnp.ones((SSM_CHUNK, SSM_CHUNK), dtype=bool))
    seg = a_cum[..., :, None] - a_cum[..., None, :]
    decay = jnp.exp(jnp.where(causal, seg, -jnp.inf))
    cb = jnp.einsum('bclgn,bcsgn->bgcls', c, b)
    y_diag = jnp.einsum('bgrcls,bcsgrp->bclgrp', decay * cb[:, :, None], x)
    decay_to_end = jnp.exp(a_cum[..., -1:] - a_cum)
    chunk_states = jnp.einsum('bclgn,bgrcl,bclgrp->bcgrpn', b, decay_to_end, x)
    chunk_decay = jnp.exp(a_cum[..., -1])

    def step(state, inp):
        new_states, dec = inp
        return state * dec[..., None, None] + new_states, state

    init = jnp.zeros((bsz, n_groups, reps, head_dim, d_state), f32)
    _, prev_states = lax.scan(step, init, (chunk_states.transpose(1, 0, 2, 3, 4, 5),
                                            chunk_decay.transpose(3, 0, 1, 2)))
    prev_states = prev_states.transpose(1, 0, 2, 3, 4, 5)
    y_off = jnp.einsum('bclgn,bcgrpn,bgrcl->bclgrp', c, prev_states, jnp.exp(a_cum))
    return (y_diag + y_off).reshape(bsz, seq, n_heads, head_dim).astype(out_dtype)


def mamba2_mixer(h, w_in, conv_w, conv_b, dt_bias, a_log, d_skip, norm_w, w_out):
    bsz, seq, _ = h.shape
    proj = h @ w_in
    z = proj[..., :SSM_D_INNER]
    xbc = proj[..., SSM_D_INNER:SSM_D_INNER + SSM_CONV_DIM]
    dt_raw = proj[..., SSM_D_INNER + SSM_CONV_DIM:]
    xbc = jax.nn.silu(causal_depthwise_conv(xbc, conv_w, conv_b))
    bc_dim = SSM_N_GROUPS * SSM_D_STATE
    xs = xbc[..., :SSM_D_INNER].reshape(bsz, seq, SSM_N_HEADS, SSM_HEAD_DIM)
    b_in = xbc[..., SSM_D_INNER:SSM_D_INNER + bc_dim].reshape(bsz, seq, SSM_N_GROUPS, SSM_D_STATE)
    c_in = xbc[..., SSM_D_INNER + bc_dim:].reshape(bsz, seq, SSM_N_GROUPS, SSM_D_STATE)
    dt = jax.nn.softplus(dt_raw.astype(jnp.float32) + dt_bias.astype(jnp.float32))
    a = -jnp.exp(a_log.astype(jnp.float32))
    y = ssd_chunked(xs * dt[..., None], dt * a, b_in, c_in)
    y = y + xs * d_skip[:, None]
    y = y.reshape(bsz, seq, SSM_D_INNER) * jax.nn.silu(z)
    group = SSM_D_INNER // SSM_N_GROUPS
    y = rms_norm(y.reshape(bsz, seq, SSM_N_GROUPS, group), norm_w.reshape(SSM_N_GROUPS, group))
    return y.reshape(bsz, seq, SSM_D_INNER) @ w_out


def apply_rotary(t, positions):
    half = ATTN_HEAD_DIM // 2
    inv_freq = ROPE_THETA ** (-jnp.arange(half, dtype=jnp.float32) / half)
    ang = positions[:, None] * inv_freq[None, :]
    cos = jnp.cos(ang)[None, :, None, None, :]
    sin = jnp.sin(ang)[None, :, None, None, :]
    tf = t.astype(jnp.float32)
    t1, t2 = tf[..., :half], tf[..., half:]
    return jnp.concatenate([t1 * cos - t2 * sin, t2 * cos + t1 * sin], axis=-1).astype(t.dtype)


def banded_window_attention(q, k, v, band):
    n, length, heads, hd = q.shape
    n_blocks = -(-length // ATTN_BLOCK)
    pad = n_blocks * ATTN_BLOCK - length

    def blocks(t):
        t = jnp.pad(t, ((0, 0), (0, pad), (0, 0), (0, 0)))
        return t.reshape(n, n_blocks, ATTN_BLOCK, heads, hd)

    def with_prev(t):
        prev = jnp.pad(t, ((0, 0), (1, 0), (0, 0), (0, 0), (0, 0)))[:, :-1]
        return jnp.concatenate([prev, t], axis=2)

    qb = blocks(q).astype(jnp.float32)
    kk = with_prev(blocks(k)).astype(jnp.float32)
    vv = with_prev(blocks(v)).astype(jnp.float32)
    scores = jnp.einsum('nbqhd,nbkhd->nbhqk', qb, kk) * (hd ** -0.5)
    qi = jnp.arange(ATTN_BLOCK)[:, None]
    kj = jnp.arange(2 * ATTN_BLOCK)[None, :]
    dist = ATTN_BLOCK + qi - kj
    blk = jnp.arange(n_blocks)[:, None, None]
    valid = (dist >= 0) & (dist <= band) & ((blk > 0) | (kj >= ATTN_BLOCK))
    scores = jnp.where(valid[None, :, None], scores, -jnp.inf)
    lse = jax.nn.logsumexp(scores, axis=-1)
    probs = jnp.exp(scores - lse[..., None])
    out = jnp.einsum('nbhqk,nbkhd->nbqhd', probs, vv)
    out = out.reshape(n, n_blocks * ATTN_BLOCK, heads, hd)[:, :length]
    lse = lse.transpose(0, 1, 3, 2).reshape(n, n_blocks * ATTN_BLOCK, heads)[:, :length]
    return out, lse


def dilated_window_attention(q, k, v, window, dilation):
    bsz, seq, heads, hd = q.shape
    length = seq // dilation

    def strided(t):
        return t.reshape(bsz, length, dilation, heads, hd).transpose(0, 2, 1, 3, 4).reshape(
            bsz * dilation, length, heads, hd)

    out, lse = banded_window_attention(strided(q), strided(k), strided(v), window // dilation)
    out = out.reshape(bsz, dilation, length, heads, hd).transpose(0, 2, 1, 3, 4).reshape(bsz, seq, heads, hd)
    lse = lse.reshape(bsz, dilation, length, heads).transpose(0, 2, 1, 3).reshape(bsz, seq, heads)
    return out, lse


def dilated_attention_mixer(h, w_qkv, q_norm, k_norm, w_out):
    bsz, seq, _ = h.shape
    qkv = (h @ w_qkv).reshape(bsz, seq, ATTN_N_GROUPS, 3, ATTN_GROUP_HEADS, ATTN_HEAD_DIM)
    positions = jnp.arange(seq, dtype=jnp.float32)
    q = apply_rotary(rms_norm(qkv[:, :, :, 0], q_norm[:, None, :]), positions)
    k = apply_rotary(rms_norm(qkv[:, :, :, 1], k_norm[:, None, :]), positions)
    v = qkv[:, :, :, 2]
    outs, lses = [], []
    for g, (window, dilation) in enumerate(ATTN_PATTERNS):
        o, lse = dilated_window_attention(q[:, :, g], k[:, :, g], v[:, :, g], window, dilation)
        outs.append(o)
        lses.append(lse)
    weights = jax.nn.softmax(jnp.stack(lses), axis=0)
    merged = jnp.einsum('gbsh,gbshd->bshd', weights, jnp.stack(outs)).astype(h.dtype)
    return merged.reshape(bsz, seq, ATTN_OUT_DIM) @ w_out


def swiglu_ffn(h, w_gate, w_up, w_down):
    return (jax.nn.silu(h @ w_gate) * (h @ w_up)) @ w_down


def setup_inputs(seed: int = 0) -> dict:
    key = jax.random.key(seed)
    ks = jax.random.split(key, 20)
    f32 = jnp.float32
    nm, na = N_SSM_LAYERS, N_ATTN_LAYERS

    def normal(k, shape, scale):
        return jax.random.normal(k, shape, f32) * scale

    dt_init = jnp.exp(jax.random.uniform(ks[6], (nm, SSM_N_HEADS), f32,
                                         minval=math.log(1e-3), maxval=math.log(1e-1)))
    return {
        'x': normal(ks[0], (BATCH, SEQ, D_MODEL), 1.0),
        'mix_norm': 1.0 + normal(ks[1], (DEPTH, D_MODEL), 0.02),
        'ffn_norm': 1.0 + normal(ks[2], (DEPTH, D_MODEL), 0.02),
        'ssm_w_in': normal(ks[3], (nm, D_MODEL, SSM_IN_DIM), D_MODEL ** -0.5),
        'ssm_conv_w': normal(ks[4], (nm, SSM_CONV_WIDTH, SSM_CONV_DIM), SSM_CONV_WIDTH ** -0.5),
        'ssm_conv_b': normal(ks[5], (nm, SSM_CONV_DIM), 0.01),
        'ssm_dt_bias': dt_init + jnp.log(-jnp.expm1(-dt_init)),
        'ssm_a_log': jnp.log(jax.random.uniform(ks[7], (nm, SSM_N_HEADS), f32, minval=1.0, maxval=16.0)),
        'ssm_d': 1.0 + normal(ks[8], (nm, SSM_N_HEADS), 0.02),
        'ssm_norm': 1.0 + normal(ks[9], (nm, SSM_D_INNER), 0.02),
        'ssm_w_out': normal(ks[10], (nm, SSM_D_INNER, D_MODEL), SSM_D_INNER ** -0.5),
        'attn_w_qkv': normal(ks[11], (na, D_MODEL, ATTN_QKV_DIM), D_MODEL ** -0.5),
        'attn_q_norm': 1.0 + normal(ks[12], (na, ATTN_N_GROUPS, ATTN_HEAD_DIM), 0.02),
        'attn_k_norm': 1.0 + normal(ks[13], (na, ATTN_N_GROUPS, ATTN_HEAD_DIM), 0.02),
        'attn_w_out': normal(ks[14], (na, ATTN_OUT_DIM, D_MODEL), ATTN_OUT_DIM ** -0.5),
        'ffn_w_gate': normal(ks[15], (DEPTH, D_MODEL, D_FF), D_MODEL ** -0.5),
        'ffn_w_up': normal(ks[16], (DEPTH, D_MODEL, D_FF), D_MODEL ** -0.5),
        'ffn_w_down': normal(ks[17], (DEPTH, D_FF, D_MODEL), D_FF ** -0.5),
    }


def reference(x, mix_norm, ffn_norm, ssm_w_in, ssm_conv_w, ssm_conv_b, ssm_dt_bias, ssm_a_log,
              ssm_d, ssm_norm, ssm_w_out, attn_w_qkv, attn_q_norm, attn_k_norm, attn_w_out,
              ffn_w_gate, ffn_w_up, ffn_w_down):
    for layer in range(DEPTH):
        j = layer // N_MIXERS
        h = rms_norm(x, mix_norm[layer])
        if layer % N_MIXERS == 0:
            mixed = mamba2_mixer(h, ssm_w_in[j], ssm_conv_w[j], ssm_conv_b[j], ssm_dt_bias[j],
                                 ssm_a_log[j], ssm_d[j], ssm_norm[j], ssm_w_out[j])
        else:
            mixed = dilated_attention_mixer(h, attn_w_qkv[j], attn_q_norm[j], attn_k_norm[j], attn_w_out[j])
        x = x + mixed
        h = rms_norm(x, ffn_norm[layer])
        x = x + swiglu_ffn(h, ffn_w_gate[layer], ffn_w_up[layer], ffn_w_down[layer])
    return x
```

```python
import numpy as np
from contextlib import ExitStack
import concourse.bass as bass
import concourse.mybir as mybir
from concourse.bass_utils import run_bass_kernel_spmd

F32 = mybir.dt.float32
BF16 = mybir.dt.bfloat16
AF = mybir.ActivationFunctionType
ALU = mybir.AluOpType
AX = mybir.AxisListType

COMPUTE = ("pe", "act", "dve", "pool")
ALLENG = ("pe", "act", "dve", "pool", "sp")

D = 2048
T = 2048
KC = D // 128
DFF = 5632
FC = DFF // 128
EPS = 1e-6


class _Op:
    __slots__ = ("eng", "fn", "dma", "key", "ndma", "pos", "cnt", "needed", "waits", "vc")


class Sched:
    def __init__(self):
        self.ops = []
        self.lw = {}
        self.rd = {}
        self.pos = {e: 0 for e in ALLENG}
        self.know = {e: {} for e in ALLENG}
        self.dma_cnt = {}

    def add(self, eng, fn, reads=(), writes=(), dma_key=None, ndma=1, phase=True):
        ops = self.ops
        i = len(ops)
        o = _Op()
        o.eng = eng
        o.fn = fn
        o.dma = dma_key is not None
        o.key = dma_key
        o.ndma = ndma
        o.needed = False
        if phase:
            reads = list(reads) + ["PHASE"]
        deps = set()
        lw = self.lw
        rd = self.rd
        for r in reads:
            w = lw.get(r)
            if w is not None:
                deps.add(w)
        for r in writes:
            w = lw.get(r)
            if w is not None:
                deps.add(w)
            d = rd.get(r)
            if d:
                deps.update(d.values())
        for r in reads:
            d = rd.get(r)
            if d is None:
                d = rd[r] = {}
            if o.dma:
                d[("dma", i)] = i
            else:
                d[eng] = i
        for r in writes:
            lw[r] = i
            rd[r] = {}
        know = self.know[eng]
        waits = []
        for p in sorted(deps):
            P = ops[p]
            if P.dma:
                if know.get(("dma", P.key), 0) >= P.cnt:
                    continue
            else:
                if P.eng == "pe" and eng == "pe" and not o.dma:
                    continue
                if know.get(P.eng, 0) >= P.pos:
                    continue
            waits.append(p)
            P.needed = True
            for kk, vv in P.vc.items():
                if know.get(kk, 0) < vv:
                    know[kk] = vv
        o.waits = waits
        if o.dma:
            c = self.dma_cnt.get(dma_key, 0) + 16 * ndma
            self.dma_cnt[dma_key] = c
            o.cnt = c
            vc = dict(know)
            vc[("dma", dma_key)] = c
            o.pos = 0
        else:
            self.pos[eng] += 1
            o.pos = self.pos[eng]
            o.cnt = 0
            vc = dict(know)
            vc[eng] = o.pos
        o.vc = vc
        ops.append(o)
        return i

    def barrier(self):
        self.add("pool", lambda e: e.memset(self.bar_ap, 0.0), reads=(), writes=["PHASE"], phase=False)

    def emit(self, nc, es):
        ops = self.ops
        sem_eng = {e: es.enter_context(nc.semaphore("s_" + e)) for e in COMPUTE}
        sem_dma = {}
        for n, k in enumerate(self.dma_cnt):
            sem_dma[k] = es.enter_context(nc.semaphore("d%d" % n))
        val = {}
        run = {e: 0 for e in COMPUTE}
        for i, o in enumerate(ops):
            if not o.dma and o.needed:
                run[o.eng] += 1
                val[i] = run[o.eng]
        self.sem_max = dict(run)
        per = {e: [] for e in ALLENG}
        for i, o in enumerate(ops):
            per[o.eng].append(i)
        engobj = {"pe": "tensor", "act": "scalar", "dve": "vector", "pool": "gpsimd", "sp": "sync"}

        def body(ename):
            def f(eng):
                for i in per[ename]:
                    o = ops[i]
                    for p in o.waits:
                        P = ops[p]
                        if P.dma:
                            eng.wait_ge(sem_dma[P.key], P.cnt)
                        else:
                            eng.wait_ge(sem_eng[P.eng], val[p])
                    if o.dma:
                        o.fn(eng, sem_dma[o.key])
                    else:
                        ins = o.fn(eng)
                        if o.needed:
                            ins.then_inc(sem_eng[ename], 1)
            return f

        with nc.Block() as block:
            for ename in ("sp", "pool", "act", "dve", "pe"):
                if per[ename]:
                    getattr(block, engobj[ename])(body(ename))


class Ctx:
    ARENA_F32 = 49 * 1024

    def __init__(self, nc, es):
        self.nc = nc
        self.S = Sched()
        self.arena = es.enter_context(nc.sbuf_tensor("arena", [128, self.ARENA_F32], F32))
        self.ps_all = es.enter_context(nc.psum_tensor("ps", [128, 4096], F32))
        self.ps = [self.ps_all[:, i * 512:(i + 1) * 512] for i in range(8)]
        self.off = 0
        self.persist = 0
        self.uid = 0
        self.S.bar_ap = self.alloc_f32(1, persist=True)

    def alloc_f32(self, n, persist=False):
        n = (n + 1) // 2 * 2
        a = self.arena[:, self.off:self.off + n]
        self.off += n
        assert self.off <= self.ARENA_F32, "SBUF arena overflow %d" % self.off
        if persist:
            assert self.persist == self.off - n
            self.persist = self.off
        return a

    def alloc_bf16(self, n, persist=False):
        return self.alloc_f32((n + 1) // 2, persist).bitcast(BF16)[:, 0:n]

    def new_phase(self):
        self.S.barrier()
        self.off = self.persist

    def ps2(self, pr):
        return self.ps_all[:, pr * 1024:(pr + 1) * 1024]

    def name(self, s):
        self.uid += 1
        return "%s#%d" % (s, self.uid)


def dma(out, in_, **kw):
    return lambda e, s: e.dma_start(out=out, in_=in_, **kw).then_inc(s, 16)


def rmsnorm_phase(C, x_d, gain, hT, tag, tile=512):
    S = C.S
    xs = [C.alloc_f32(KC * tile).rearrange("p (k n) -> p k n", k=KC) for _ in range(2)]
    sq = C.alloc_f32(KC * tile).rearrange("p (k n) -> p k n", k=KC)
    rstd = [C.alloc_f32(tile) for _ in range(2)]
    ones = C.ones_f32
    for tt in range(T // tile):
        b = tt % 2
        xsb = xs[b]
        hres = ("hT", (tt * tile) // 512)
        S.add("sp", dma(xsb, x_d[:, tt * tile:(tt + 1) * tile].rearrange("(k p) n -> p k n", p=128)),
              writes=[("xs", b)], dma_key=("xs", b))
        S.add("act", lambda e, xsb=xsb: e.activation(out=sq, in_=xsb, func=AF.Square), reads=[("xs", b)], writes=["sq"])
        for k in range(KC):
            S.add("pe", lambda e, k=k: e.matmul(C.ps[7][:, 0:tile], ones, sq[:, k, :], start=(k == 0), stop=(k == KC - 1)), reads=["sq"], writes=["ps7"])
        S.add("act", lambda e, b=b: e.activation(out=rstd[b], in_=C.ps[7][:, 0:tile], func=AF.Sqrt, scale=1.0 / D, bias=C.eps_ap),
              reads=["ps7"], writes=[("rstd", b)])
        S.add("dve", lambda e, b=b: e.reciprocal(out=rstd[b], in_=rstd[b]), reads=[("rstd", b)], writes=[("rstd", b)])
        for k in range(KC):
            S.add("dve", lambda e, k=k, b=b, xsb=xsb, tt=tt: e.scalar_tensor_tensor(
                out=hT[:, k, tt * tile:(tt + 1) * tile], in0=xsb[:, k, :], scalar=gain[:, k:k + 1], in1=rstd[b],
                op0=ALU.mult, op1=ALU.mult), reads=[("xs", b), ("rstd", b)], writes=[hres])


def gemm(C, W_list, kch, actT, act_reads, ncols_total, gcols, wbufs, epilogue, th_list=(0, 1), tcols=1024, pairs=(0, 1, 2, 3)):
    S = C.S
    nW = len(W_list)
    ngroups = (ncols_total + gcols - 1) // gcols
    nbuf = len(wbufs[0])
    pp_ctr = [0]

    def load(g):
        c0 = g * gcols
        nc_ = min(gcols, ncols_total - c0)
        for wi in range(nW):
            slot = g % nbuf
            S.add("pool", dma(wbufs[wi][slot][:, :, 0:nc_], W_list[wi][:, c0:c0 + nc_].rearrange("(k p) n -> p k n", p=128)),
                  writes=[("w", wi, slot)], dma_key=("w", wi, slot))

    for g in range(min(nbuf - 1, ngroups)):
        load(g)
    for g in range(ngroups):
        if g + nbuf - 1 < ngroups:
            load(g + nbuf - 1)
        c0 = g * gcols
        nc_ = min(gcols, ncols_total - c0)
        slot = g % nbuf
        for mb in range((nc_ + 127) // 128):
            mw = min(128, nc_ - mb * 128)
            blk = (c0 + mb * 128) // 128
            for th in th_list:
                prs = []
                for wi in range(nW):
                    pr = pairs[pp_ctr[0] % len(pairs)]
                    pp_ctr[0] += 1
                    prs.append(pr)
                    wb = wbufs[wi][slot]
                    for k in range(kch):
                        for t2 in range(tcols // 512):
                            S.add("pe", lambda e, pr=pr, wb=wb, k=k, t2=t2, mb=mb, mw=mw, th=th: e.matmul(
                                C.ps[2 * pr + t2][0:mw, :], wb[:, k, mb * 128:mb * 128 + mw],
                                actT[:, k, th * tcols + t2 * 512: th * tcols + (t2 + 1) * 512],
                                start=(k == 0), stop=(k == kch - 1)),
                                reads=[("w", wi, slot)] + act_reads, writes=[("pp", pr)])
                epilogue(blk, th, prs, mw)


def ffn_sublayer(C, x_in_d, x_out_d, gain, wg_d, wu_d, wd_d, act_d, out_key):
    S = C.S
    C.new_phase()
    hT = C.alloc_bf16(KC * T).rearrange("p (k n) -> p k n", k=KC)
    mark = C.off
    rmsnorm_phase(C, x_in_d, gain, hT, "ffn")
    S.barrier()
    C.off = mark
    wb = [[C.alloc_bf16(KC * 512).rearrange("p (k n) -> p k n", k=KC) for _ in range(3)] for _ in range(2)]
    sg = [C.alloc_f32(1024) for _ in range(2)]
    ao = [C.alloc_bf16(1024) for _ in range(2)]
    ectr = [0]

    def epi(blk, th, prs, mw):
        b = ectr[0] % 2
        ectr[0] += 1
        g_ps = C.ps2(prs[0])
        u_ps = C.ps2(prs[1])
        S.add("act", lambda e: e.activation(out=sg[b], in_=g_ps, func=AF.Silu), reads=[("pp", prs[0])], writes=[("sg", b)])
        S.add("dve", lambda e: e.tensor_tensor(out=ao[b], in0=sg[b], in1=u_ps, op=ALU.mult),
              reads=[("sg", b), ("pp", prs[1])], writes=[("ao", b)])
        S.add("sp", dma(act_d[blk * 128:(blk + 1) * 128, th * 1024:(th + 1) * 1024], ao[b]),
              reads=[("ao", b)], writes=[("actd", th)], dma_key=("ao", b))

    gemm(C, [wg_d, wu_d], KC, hT, [("hT", i) for i in range(4)], DFF, 512, wb, epi)
    C.new_phase()
    aT = C.alloc_bf16(FC * 1024).rearrange("p (k n) -> p k n", k=FC)
    wdb = [[C.alloc_bf16(FC * 256).rearrange("p (k n) -> p k n", k=FC) for _ in range(3)]]
    xr = [C.alloc_f32(1024) for _ in range(2)]
    xo = [C.alloc_f32(1024) for _ in range(2)]
    for th in range(2):
        for q in range(4):
            k0, k1 = q * FC // 4, (q + 1) * FC // 4
            S.add("sp", dma(aT[:, k0:k1, :], act_d[k0 * 128:k1 * 128, th * 1024:(th + 1) * 1024].rearrange("(k p) n -> p k n", p=128)),
                  reads=[("actd", th)], writes=[("aT", q)], dma_key=("aT", q))
        ectr2 = [0]

        def epi2(blk, th_, prs, mw, th=th):
            b = ectr2[0] % 2
            ectr2[0] += 1
            S.add("act", dma(xr[b], x_in_d[blk * 128:(blk + 1) * 128, th * 1024:(th + 1) * 1024]),
                  writes=[("xr", b)], dma_key=("xr", b))
            S.add("dve", lambda e: e.tensor_tensor(out=xo[b], in0=xr[b], in1=C.ps2(prs[0]), op=ALU.add),
                  reads=[("xr", b), ("pp", prs[0])], writes=[("xo", b)])
            S.add("sp", dma(x_out_d[blk * 128:(blk + 1) * 128, th * 1024:(th + 1) * 1024], xo[b]),
                  reads=[("xo", b)], writes=[(out_key, b)], dma_key=("xo", b))

        gemm(C, [wd_d], FC, aT, [("aT", q) for q in range(4)], D, 256, wdb, epi2, th_list=(0,))


CST = {}
_o = 0
for _n, _w in (("mixn0", 16), ("ffnn0", 16), ("mixn1", 16), ("ffnn1", 16), ("convw", 192), ("convb", 48),
               ("dtb", 1), ("alog", 1), ("dskip", 64), ("ssmn", 32), ("ident", 128), ("tri", 128),
               ("qkn", 6), ("maskA", 128), ("maskB", 128), ("ones", 128)):
    CST[_n] = (_o, _o + _w)
    _o += _w
NCST = _o

SSM_DI = 4096
NH = 64


def resid_epilogue(C, x_in_d, x_out_d, out_key, th_of=None):
    S = C.S
    xr = [C.alloc_f32(1024) for _ in range(2)]
    xo = [C.alloc_f32(1024) for _ in range(2)]
    ctr = [0]

    def epi(blk, th, prs, mw):
        if th_of is not None:
            th = th_of[0]
        b = ctr[0] % 2
        ctr[0] += 1
        S.add("act", dma(xr[b], x_in_d[blk * 128:(blk + 1) * 128, th * 1024:(th + 1) * 1024]),
              writes=[("xr", b)], dma_key=("xr", b))
        S.add("dve", lambda e: e.tensor_tensor(out=xo[b], in0=xr[b], in1=C.ps2(prs[0]), op=ALU.add),
              reads=[("xr", b), ("pp", prs[0])], writes=[("xo", b)])
        S.add("sp", dma(x_out_d[blk * 128:(blk + 1) * 128, th * 1024:(th + 1) * 1024], xo[b]),
              reads=[("xo", b)], writes=[(out_key, b)], dma_key=("xo", b))
    return epi


def mamba_sublayer(C, x_in_d, x_out_d, gain, w_in_d, w_out_d, sp, out_key, stop_after=None):
    S = C.S
    cst = C.cst

    def cc(name, a=None, b=None):
        o0, o1 = CST[name]
        if a is None:
            return cst[:, o0:o1]
        return cst[:, o0 + a:o0 + b]

    C.new_phase()
    dtT = C.alloc_f32(T)
    aT = C.alloc_f32(T)
    aneg = C.alloc_f32(2)
    keep = C.off
    hT = C.alloc_bf16(KC * T).rearrange("p (k n) -> p k n", k=KC)
    mark = C.off
    rmsnorm_phase(C, x_in_d, gain, hT, "mix0")
    S.barrier()
    C.off = mark
    wb = [[C.alloc_bf16(KC * 512).rearrange("p (k n) -> p k n", k=KC) for _ in range(3)]]
    ub = [C.alloc_f32(1028) for _ in range(2)]
    acc = [C.alloc_f32(1024) for _ in range(2)]
    so = [C.alloc_f32(1024) for _ in range(2)]
    S.add("act", lambda e: e.activation(out=aneg[0:64, 0:1], in_=cc("alog")[0:64, :], func=AF.Exp), reads=["cst"], writes=["aneg"])
    S.add("dve", lambda e: e.tensor_scalar(out=aneg[0:64, 0:1], in0=aneg[0:64, 0:1], scalar1=-1.0, scalar2=None, op0=ALU.mult),
          reads=["aneg"], writes=["aneg"])
    ctr = [0]

    def epi(blk, th, prs, mw):
        b = ctr[0] % 2
        ctr[0] += 1
        pp = ("pp", prs[0])
        ps = C.ps2(prs[0])
        cols = slice(th * 1024, (th + 1) * 1024)
        if blk < 32:
            S.add("act", lambda e: e.activation(out=so[b], in_=ps, func=AF.Silu), reads=[pp], writes=[("so", b)])
            S.add("sp", dma(sp["zs"][blk // 8][(blk % 8) * 128:(blk % 8 + 1) * 128, cols], so[b]), reads=[("so", b)], writes=["zs"], dma_key=("so", b))
        elif blk < 80:
            j = blk - 32
            if j < 32:
                dst = sp["xs"][j // 8][(j % 8) * 128:(j % 8 + 1) * 128, cols]
                dkey = "xs"
            elif j < 40:
                dst = sp["bs"][(j - 32) * 128:(j - 31) * 128, cols]
                dkey = "bs"
            else:
                dst = sp["cs"][(j - 40) * 128:(j - 39) * 128, cols]
                dkey = "cs"
            u = ub[b]
            S.add("act", lambda e: e.activation(out=u[:, 3:1027], in_=ps, func=AF.Copy), reads=[pp], writes=[("ub", b)])
            if th == 0:
                S.add("pool", lambda e: e.memset(u[:, 0:3], 0.0), writes=[("ubh", b)])
            else:
                S.add("pool", lambda e: e.tensor_copy(out=u[:, 0:3], in_=ub[1 - b][:, 1024:1027]), reads=[("ub", 1 - b)], writes=[("ubh", b)])
            S.add("act", lambda e: e.activation(out=acc[b], in_=ps, func=AF.Identity, scale=cc("convw", 4 * j + 3, 4 * j + 4),
                                                bias=cc("convb", j, j + 1)), reads=[pp, "cst"], writes=[("acc", b)])
            for k in (2, 1, 0):
                S.add("dve", lambda e, k=k: e.scalar_tensor_tensor(out=acc[b], in0=u[:, k:k + 1024], scalar=cc("convw", 4 * j + k, 4 * j + k + 1),
                                                                  in1=acc[b], op0=ALU.mult, op1=ALU.add),
                      reads=[("ub", b), ("ubh", b), ("acc", b)], writes=[("acc", b)])
            S.add("act", lambda e: e.activation(out=so[b], in_=acc[b], func=AF.Silu), reads=[("acc", b)], writes=[("so", b)])
            S.add("sp", dma(dst, so[b]), reads=[("so", b)], writes=[dkey], dma_key=("so", b))
        else:
            S.add("act", lambda e: e.activation(out=so[b][0:64, :], in_=ps[0:64, :], func=AF.Exp, bias=cc("dtb")[0:64, :]),
                  reads=[pp, "cst"], writes=[("so", b)])
            S.add("act", lambda e: e.activation(out=dtT[0:64, cols], in_=so[b][0:64, :], func=AF.Ln, bias=cc("ones", 0, 1)[0:64, :]),
                  reads=[("so", b), "cst"], writes=["dtT"])
            S.add("dve", lambda e: e.tensor_scalar(out=aT[0:64, cols], in0=dtT[0:64, cols], scalar1=aneg[0:64, 0:1], scalar2=None, op0=ALU.mult),
                  reads=["dtT", "aneg"], writes=["aT"])

    import os
    if not os.environ.get("MAMBA_SKIPA"):
        gemm(C, [w_in_d], KC, hT, [("hT", i) for i in range(4)], 10304, 512, wb, epi)

    if stop_after == "A":
        return
    S.barrier()
    C.off = keep
    state = C.alloc_f32(SSM_DI)
    state_bf = C.alloc_bf16(SSM_DI)
    xc = [C.alloc_f32(32 * 128).rearrange("p (k n) -> p k n", k=32) for _ in range(2)]
    zcs = [C.alloc_f32(32 * 128).rearrange("p (k n) -> p k n", k=32) for _ in range(2)]
    bc = [C.alloc_f32(8 * 128).rearrange("p (k n) -> p k n", k=8) for _ in range(2)]
    cc_ = [C.alloc_f32(8 * 128).rearrange("p (k n) -> p k n", k=8) for _ in range(2)]
    bbfs = [C.alloc_bf16(8 * 128).rearrange("p (k n) -> p k n", k=8) for _ in range(2)]
    cbfs = [C.alloc_bf16(8 * 128).rearrange("p (k n) -> p k n", k=8) for _ in range(2)]
    ynb = [C.alloc_bf16(32 * 128).rearrange("p (k n) -> p k n", k=32) for _ in range(2)]
    acTs = [C.alloc_f32(128) for _ in range(2)]
    ac2s = [C.alloc_bf16(128) for _ in range(2)]
    acr = C.alloc_f32(128)
    sq1 = [C.alloc_f32(128) for _ in range(2)]
    dteT = C.alloc_f32(128)
    acl = C.alloc_f32(2)
    ac_tm = C.alloc_f32(64)
    nac_tms = [C.alloc_f32(64) for _ in range(2)]
    dt_tms = [C.alloc_f32(64) for _ in range(2)]
    wst_tms = [C.alloc_f32(64) for _ in range(2)]
    eac_tms = [C.alloc_f32(64) for _ in range(2)]
    cdd = C.alloc_f32(64)
    cd_bs = [C.alloc_f32(64) for _ in range(2)]
    xdt = [C.alloc_bf16(512) for _ in range(2)]
    xw = [C.alloc_bf16(512) for _ in range(2)]
    xD = [C.alloc_f32(512) for _ in range(2)]
    btm = [C.alloc_bf16(128) for _ in range(2)]
    cbm = [C.alloc_f32(128) for _ in range(2)]
    E = [C.alloc_f32(512) for _ in range(2)]
    MT = [C.alloc_bf16(1024).rearrange("p (h l) -> p h l", h=8) for _ in range(2)]
    yt = [C.alloc_f32(512) for _ in range(2)]
    ygT = [C.alloc_f32(512) for _ in range(2)]
    sqb = [C.alloc_f32(512) for _ in range(2)]
    rs = [C.alloc_f32(128) for _ in range(2)]
    ident = cc("ident")
    ones = cc("ones")
    P = C.ps
    import os
    NCH = int(os.environ.get('MAMBA_NCH', T // 128))
    DBG = float(os.environ.get('MAMBA_DBG', '99'))

    def load_chunk(c):
        b2 = c % 2
        tcols = slice(c * 128, (c + 1) * 128)
        def ld4(dst, src):
            def f(e, s_):
                for q4 in range(4):
                    e.dma_start(out=dst[:, 8 * q4:8 * q4 + 8, :], in_=src[q4 * 1024:(q4 + 1) * 1024, tcols].rearrange("(k p) n -> p k n", p=128)).then_inc(s_, 16)
            return f
        for q4 in range(4):
            S.add("sp", dma(xc[b2][:, 8 * q4:8 * q4 + 8, :], sp["xs"][q4][:, tcols].rearrange("(k p) n -> p k n", p=128)),
                  reads=["xs"], writes=[("xc", b2, q4)], dma_key=("xc", b2, q4))
        S.add("sp", dma(bc[b2], sp["bs"][:, tcols].rearrange("(k p) n -> p k n", p=128)), reads=["bs"], writes=[("bc", b2)], dma_key=("bc", b2))
        S.add("sp", dma(cc_[b2], sp["cs"][:, tcols].rearrange("(k p) n -> p k n", p=128)), reads=["cs"], writes=[("cc", b2)], dma_key=("cc", b2))
        for q4 in range(4):
            S.add("sp", dma(zcs[b2][:, 8 * q4:8 * q4 + 8, :], sp["zs"][q4][:, tcols].rearrange("(k p) n -> p k n", p=128)),
                  reads=["zs"], writes=[("zc", b2, q4)], dma_key=("zc", b2, q4))

    load_chunk(0)
    def do_chunk(c):
        b2 = c % 2
        tcols = slice(c * 128, (c + 1) * 128)
        zc = zcs[b2]
        bbf = bbfs[b2]
        cbf = cbfs[b2]
        acT = acTs[b2]
        nac_tm = nac_tms[b2]
        dt_tm = dt_tms[b2]
        wst_tm = wst_tms[b2]
        eac_tm = eac_tms[b2]
        cd_b = cd_bs[b2]
        S.add("dve", lambda e, tcols=tcols: e.tensor_tensor_scan(out=acT[0:64, :], data0=ones[0:64, :], data1=aT[0:64, tcols], initial=0.0,
                                                                 op0=ALU.mult, op1=ALU.add), reads=["aT", "cst"], writes=[("acT", b2)])
        ac2 = ac2s[b2]
        S.add("dve", lambda e: e.tensor_copy(out=ac2[0:64, :], in_=acT[0:64, :]), reads=[("acT", b2)], writes=[("ac3", b2)])
        S.add("dve", lambda e: e.tensor_tensor(out=acr[0:64, :], in0=acT[0:64, :], in1=ac2[0:64, :], op=ALU.subtract), reads=[("acT", b2), ("ac3", b2)], writes=["acr"])
        S.add("dve", lambda e: e.tensor_copy(out=ac2[64:128, :], in_=acr[0:64, :]), reads=["acr"], writes=[("ac3", b2)])
        S.add("dve", lambda e: e.tensor_scalar(out=acl[0:64, 0:1], in0=acT[0:64, 127:128], scalar1=1.0, scalar2=None, op0=ALU.mult),
              reads=[("acT", b2)], writes=["aclast"])
        S.add("act", lambda e: e.activation(out=dteT[0:64, :], in_=acT[0:64, :], func=AF.Exp, scale=-1.0, bias=acl[0:64, 0:1]),
              reads=[("acT", b2), "aclast"], writes=["dteT"])
        S.add("pe", lambda e: e.transpose(P[0][:, 0:64], acT[0:64, :], ident[0:64, 0:64]), reads=[("acT", b2), "cst"], writes=["p0"])
        S.add("pe", lambda e, tcols=tcols: e.transpose(P[0][:, 64:128], dtT[0:64, tcols], ident[0:64, 0:64]), reads=["dtT", "cst"], writes=["p0"])
        S.add("pe", lambda e: e.transpose(P[0][:, 128:192], dteT[0:64, :], ident[0:64, 0:64]), reads=["dteT", "cst"], writes=["p0"])
        S.add("dve", lambda e: e.tensor_scalar(out=cdd[0:64, :], in0=ident[0:64, 0:64], scalar1=acl[0:64, 0:1], scalar2=None, op0=ALU.mult),
              reads=["aclast", "cst"], writes=["cdd"])
        S.add("pe", lambda e: e.matmul(P[0][:, 192:256], ones[0:64, :], cdd[0:64, :], start=True, stop=True), reads=["cdd", "cst"], writes=["p0"])
        S.add("act", lambda e: e.activation(out=ac_tm, in_=P[0][:, 0:64], func=AF.Copy), reads=["p0"], writes=["ac_tm"])
        S.add("act", lambda e: e.activation(out=nac_tm, in_=P[0][:, 0:64], func=AF.Identity, scale=-1.0), reads=["p0"], writes=[("nac_tm", b2)])
        S.add("act", lambda e: e.activation(out=eac_tm, in_=P[0][:, 0:64], func=AF.Exp), reads=["p0"], writes=[("eac_tm", b2)])
        S.add("act", lambda e: e.activation(out=dt_tm, in_=P[0][:, 64:128], func=AF.Copy), reads=["p0"], writes=[("dt_tm", b2)])
        S.add("dve", lambda e: e.tensor_tensor(out=wst_tm, in0=dt_tm, in1=P[0][:, 128:192], op=ALU.mult), reads=[("dt_tm", b2), "p0"], writes=[("wst_tm", b2)])
        S.add("act", lambda e: e.activation(out=cd_b, in_=P[0][:, 192:256], func=AF.Exp), reads=["p0"], writes=[("cd_b", b2)])
        S.add("pool", lambda e, b2=b2: e.tensor_copy(out=bbf, in_=bc[b2]), reads=[("bc", b2)], writes=[("bbf", b2)])
        S.add("pool", lambda e, b2=b2: e.tensor_copy(out=cbf, in_=cc_[b2]), reads=[("cc", b2)], writes=[("cbf", b2)])
        def s1(g):
            q = g % 2
            hs = slice(8 * g, 8 * g + 8)
            bx = 1 + q
            for i in range(4):
                S.add("pe", lambda e, i=i, g=g, bx=bx, b2=b2: e.transpose(P[bx][:, i * 128:(i + 1) * 128], xc[b2][:, 4 * g + i, :], ident),
                      reads=[("xc", b2, g // 2), "cst"], writes=[("px", q)])
            x3 = P[bx].rearrange("p (h d) -> p h d", h=8)
            S.add("dve", lambda e, q=q, x3=x3, hs=hs: e.tensor_tensor(out=xdt[q].rearrange("p (h d) -> p h d", h=8), in0=x3,
                                                                     in1=dt_tm[:, hs].unsqueeze(2).broadcast_to([128, 8, 64]), op=ALU.mult),
                  reads=[("px", q), ("dt_tm", b2)], writes=[("xdt", q)])
            S.add("dve", lambda e, q=q, x3=x3, hs=hs: e.tensor_tensor(out=xw[q].rearrange("p (h d) -> p h d", h=8), in0=x3,
                                                                     in1=wst_tm[:, hs].unsqueeze(2).broadcast_to([128, 8, 64]), op=ALU.mult),
                  reads=[("px", q), ("wst_tm", b2)], writes=[("xw", q)])
            S.add("dve", lambda e, q=q, x3=x3, hs=hs: e.tensor_tensor(out=xD[q].rearrange("p (h d) -> p h d", h=8), in0=x3,
                                                                     in1=cc("dskip")[:, hs].unsqueeze(2).broadcast_to([128, 8, 64]), op=ALU.mult),
                  reads=[("px", q), "cst"], writes=[("xD", q)])
            S.add("pe", lambda e, g=g, b2=b2: e.transpose(P[3][:, 0:128], bc[b2][:, g, :], ident), reads=[("bc", b2), "cst"], writes=["p3"])
            S.add("act", lambda e, q=q: e.activation(out=btm[q], in_=P[3][:, 0:128], func=AF.Copy), reads=["p3"], writes=[("btm", q)])
            S.add("pe", lambda e, g=g: e.matmul(P[0][:, 256:384], bbf[:, g, :], cbf[:, g, :], start=True, stop=True), reads=[("bbf", b2), ("cbf", b2)], writes=["p0"])
            S.add("dve", lambda e, q=q: e.tensor_tensor(out=cbm[q], in0=P[0][:, 256:384], in1=cc("tri"), op=ALU.mult), reads=["p0", "cst"], writes=[("cbm", q)])
            for hh in range(2):
                r = hh
                br = 4 + r
                S.add("pe", lambda e, br=br: e.matmul(P[br][:, :], C.ident_bf, C.maskA4_bf, start=True, stop=False),
                      reads=["cbf_ident", "cbf_maskA"], writes=[("pr", r)])
                for i in range(4):
                    h = 8 * g + 4 * hh + i
                    S.add("pe", lambda e, h=h, i=i, br=br: e.matmul(P[br][:, i * 128:(i + 1) * 128], C.sel2_bf[:, h:h + 1].broadcast_to([128, 128]),
                                                                     ac2, start=False, stop=(i == 3)),
                          reads=[("ac3", b2), "cbf_sel2"], writes=[("pr", r)])
                for i in range(4):
                    h = 8 * g + 4 * hh + i
                    S.add("act", lambda e, h=h, i=i, br=br, r=r: e.activation(out=E[r][:, i * 128:(i + 1) * 128], in_=P[br][:, i * 128:(i + 1) * 128],
                                                                            func=AF.Exp, bias=nac_tm[:, h:h + 1]), reads=[("pr", r), ("nac_tm", b2)], writes=[("E", r)])
                S.add("pool", lambda e, q=q, hh=hh, r=r: e.tensor_tensor(
                    out=MT[q][:, 4 * hh:4 * hh + 4, :], in0=E[r].rearrange("p (h l) -> p h l", h=4),
                    in1=cbm[q].unsqueeze(1).broadcast_to([128, 4, 128]), op=ALU.mult),
                    reads=[("E", r), ("cbm", q)], writes=[("MT", q)])
        def s2(g):
            q = g % 2
            hs = slice(8 * g, 8 * g + 8)
            bx = 1 + q
            for i in range(8):
                S.add("pe", lambda e, i=i, q=q: e.matmul(P[6][:, i * 64:(i + 1) * 64], MT[q][:, i, :], xdt[q][:, i * 64:(i + 1) * 64], start=True, stop=True),
                      reads=[("MT", q), ("xdt", q)], writes=["p6"])
            if c > 0:
                S.add("pe", lambda e, g=g: e.matmul(P[7][:, :], cbf[:, g, :], state_bf[:, g * 512:(g + 1) * 512], start=True, stop=True),
                      reads=[("cbf", b2), ("sbf", g)], writes=["p7"])
                S.add("dve", lambda e, q=q, hs=hs: e.tensor_tensor(out=yt[q].rearrange("p (h d) -> p h d", h=8), in0=P[7].rearrange("p (h d) -> p h d", h=8),
                                                                  in1=eac_tm[:, hs].unsqueeze(2).broadcast_to([128, 8, 64]), op=ALU.mult),
                      reads=["p7", ("eac_tm", b2)], writes=[("yt", q)])
                S.add("dve", lambda e, q=q: e.tensor_tensor(out=yt[q], in0=yt[q], in1=P[6][:, :], op=ALU.add), reads=[("yt", q), "p6"], writes=[("yt", q)])
                S.add("pool", lambda e, q=q: e.tensor_tensor(out=yt[q], in0=yt[q], in1=xD[q], op=ALU.add), reads=[("yt", q), ("xD", q)], writes=[("yt", q)])
            else:
                S.add("dve", lambda e, q=q: e.tensor_tensor(out=yt[q], in0=xD[q], in1=P[6][:, :], op=ALU.add), reads=[("xD", q), "p6"], writes=[("yt", q)])
            if c < NCH - 1:
                S.add("pe", lambda e, q=q: e.matmul(P[7][:, :], btm[q], xw[q], start=True, stop=True), reads=[("btm", q), ("xw", q)], writes=["p7"])
                sg_ = state[:, g * 512:(g + 1) * 512]
                if c > 0:
                    S.add("pool", lambda e, sg_=sg_, hs=hs: e.tensor_tensor(out=sg_.rearrange("p (h d) -> p h d", h=8), in0=sg_.rearrange("p (h d) -> p h d", h=8),
                                                                          in1=cd_b[:, hs].unsqueeze(2).broadcast_to([128, 8, 64]), op=ALU.mult),
                          reads=[("st", g), ("cd_b", b2)], writes=[("st", g)])
                    S.add("dve", lambda e, sg_=sg_: e.tensor_tensor(out=sg_, in0=sg_, in1=P[7][:, :], op=ALU.add), reads=[("st", g), "p7"], writes=[("st", g)])
                else:
                    S.add("dve", lambda e, sg_=sg_: e.tensor_copy(out=sg_, in_=P[7][:, :]), reads=["p7"], writes=[("st", g)])
                S.add("act", lambda e, sg_=sg_, g=g: e.activation(out=state_bf[:, g * 512:(g + 1) * 512], in_=sg_, func=AF.Copy),
                      reads=[("st", g)], writes=[("sbf", g)])
        def s3(g):
            q = g % 2
            bx = 1 + q
            for i in range(4):
                S.add("pe", lambda e, i=i, q=q, bx=bx: e.transpose(P[bx][:, i * 128:(i + 1) * 128], yt[q][:, i * 128:(i + 1) * 128], ident),
                      reads=[("yt", q), "cst"], writes=[("px", q)])
            S.add("dve", lambda e, q=q, g=g, bx=bx: e.tensor_tensor(out=ygT[q].rearrange("p (k n) -> p k n", k=4), in0=P[bx].rearrange("p (k n) -> p k n", k=4),
                                                                   in1=zc[:, 4 * g:4 * g + 4, :], op=ALU.mult), reads=[("px", q), ("zc", b2, g // 2)], writes=[("ygT", q)])
            S.add("act", lambda e, q=q: e.activation(out=sqb[q], in_=ygT[q], func=AF.Square), reads=[("ygT", q)], writes=[("sqb", q)])

        def s4(g):
            q = g % 2
            S.add("dve", lambda e, q=q: e.tensor_reduce(out=sq1[q], in_=sqb[q].rearrange("p (k n) -> p n k", k=4), axis=AX.X, op=ALU.add),
                  reads=[("sqb", q)], writes=[("sq1", q)])
            S.add("pe", lambda e, q=q: e.matmul(P[3][:, 384:512], ones, sq1[q], start=True, stop=True), reads=[("sq1", q), "cst"], writes=["p3"])
            S.add("act", lambda e, q=q: e.activation(out=rs[q], in_=P[3][:, 384:512], func=AF.Ln, scale=1.0 / 512, bias=C.eps_ap), reads=["p3"], writes=[("rs", q)])
            S.add("act", lambda e, q=q: e.activation(out=rs[q], in_=rs[q], func=AF.Exp, scale=-0.5), reads=[("rs", q)], writes=[("rs", q)])
            for i in range(4):
                S.add("dve", lambda e, i=i, q=q, g=g, b2=b2: e.scalar_tensor_tensor(
                    out=ynb[b2][:, 4 * g + i, :], in0=ygT[q][:, i * 128:(i + 1) * 128], scalar=cc("ssmn", 4 * g + i, 4 * g + i + 1), in1=rs[q],
                    op0=ALU.mult, op1=ALU.mult), reads=[("ygT", q), ("rs", q), "cst"], writes=[("ynb", b2)])
        def store():
            S.add("sp", dma(sp["yn"][:, tcols].rearrange("(k p) n -> p k n", p=128), ynb[b2]), reads=[("ynb", b2)], writes=["yn"], dma_key=("ynb", b2))
        return s1, s2, s3, s4, store

    if NCH > 1:
        load_chunk(1)
    st = {0: do_chunk(0)}
    seq = [(c, g) for c in range(NCH) for g in range(8)]
    st[0][0](0)
    for idx, (c, g) in enumerate(seq):
        if g == 3 and c + 1 < NCH:
            st[c + 1] = do_chunk(c + 1)
        st[c][1](g)
        if idx + 1 < len(seq):
            c1, g1 = seq[idx + 1]
            st[c1][0](g1)
        st[c][2](g)
        if g == 7 and c + 2 < NCH:
            load_chunk(c + 2)
        if idx > 0:
            c0, g0 = seq[idx - 1]
            st[c0][3](g0)
            if g0 == 7:
                st[c0][4]()
    cl, gl = seq[-1]
    st[cl][3](gl)
    st[cl][4]()

    if stop_after == "B":
        return
    C.new_phase()
    yT = C.alloc_bf16(32 * T).rearrange("p (k n) -> p k n", k=32)
    wob = [[C.alloc_bf16(32 * 256).rearrange("p (k n) -> p k n", k=32) for _ in range(2)]]
    for q4 in range(4):
        S.add("sp", dma(yT[:, 8 * q4:8 * q4 + 8, :], sp["yn"][q4 * 1024:(q4 + 1) * 1024, :].rearrange("(k p) n -> p k n", p=128)),
              reads=["yn"], writes=[("yT", q4)], dma_key=("yT", q4))
    epi3 = resid_epilogue(C, x_in_d, x_out_d, out_key)
    gemm(C, [w_out_d], 32, yT, [("yT", q4) for q4 in range(4)], D, 256, wob, epi3)


def _cols(v, n):
    return np.ascontiguousarray(np.asarray(v, np.float32).reshape(n, 128).T)


def make_cst(inp):
    c = np.zeros((128, NCST), np.float32)

    def put(name, arr):
        o0, o1 = CST[name]
        arr = np.asarray(arr, np.float32)
        c[0:arr.shape[0], o0:o0 + arr.shape[1]] = arr

    put("mixn0", _cols(inp["mix_norm"][0], 16))
    put("ffnn0", _cols(inp["ffn_norm"][0], 16))
    put("mixn1", _cols(inp["mix_norm"][1], 16))
    put("ffnn1", _cols(inp["ffn_norm"][1], 16))
    cw = np.asarray(inp["ssm_conv_w"][0], np.float32)
    put("convw", cw.reshape(4, 48, 128).transpose(2, 1, 0).reshape(128, 192))
    put("convb", _cols(inp["ssm_conv_b"][0], 48))
    put("dtb", np.asarray(inp["ssm_dt_bias"][0], np.float32).reshape(64, 1))
    put("alog", np.asarray(inp["ssm_a_log"][0], np.float32).reshape(64, 1))
    put("dskip", np.broadcast_to(np.asarray(inp["ssm_d"][0], np.float32)[None, :], (128, 64)))
    put("ssmn", _cols(inp["ssm_norm"][0], 32))
    put("ident", np.eye(128, dtype=np.float32))
    put("tri", np.triu(np.ones((128, 128), np.float32)))
    qk = np.stack([np.asarray(inp["attn_q_norm"][0], np.float32), np.asarray(inp["attn_k_norm"][0], np.float32)], 1)
    put("qkn", qk.reshape(6, 128).T)
    NEG = -30000.0
    kk = np.arange(128)[:, None]
    qq = np.arange(128)[None, :]
    put("maskA", np.where(qq >= kk, 0.0, NEG))
    put("maskB", np.where(qq <= kk, 0.0, NEG))
    put("ones", np.ones((128, 128), np.float32))
    return c


def setup_consts(C, cst_d):
    S = C.S
    C.cst = C.alloc_f32(NCST, persist=True)
    C.eps_ap = C.alloc_f32(2, persist=True)[:, 0:1]
    S.add("sp", dma(C.cst, cst_d), writes=["cst"], dma_key="cst")
    S.add("pool", lambda e: e.memset(C.eps_ap, EPS), writes=["eps"])
    o0, o1 = CST["ones"]
    C.ones_f32 = C.cst[:, o0:o1]

    def cs(name):
        a, b = CST[name]
        return C.cst[:, a:b]
    C.ident_bf = C.alloc_bf16(128, persist=True)
    C.ones_bf = C.alloc_bf16(128, persist=True)
    C.maskA4_bf = C.alloc_bf16(512, persist=True)
    C.maskB_bf = C.alloc_bf16(128, persist=True)
    S.add("pool", lambda e: e.tensor_copy(out=C.ident_bf, in_=cs("ident")), reads=["cst"], writes=["cbf_ident"])
    S.add("pool", lambda e: e.tensor_copy(out=C.ones_bf, in_=cs("ones")), reads=["cst"], writes=["cbf_ones"])
    S.add("pool", lambda e: e.tensor_copy(out=C.maskA4_bf.rearrange("p (a b) -> p a b", a=4), in_=cs("maskA").unsqueeze(1).broadcast_to([128, 4, 128])),
          reads=["cst"], writes=["cbf_maskA"])
    S.add("pool", lambda e: e.tensor_copy(out=C.maskB_bf, in_=cs("maskB")), reads=["cst"], writes=["cbf_maskB"])
    C.sel2_bf = C.alloc_bf16(64, persist=True)
    S.add("pool", lambda e: e.tensor_copy(out=C.sel2_bf[0:64, :], in_=cs("ident")[0:64, 0:64]), reads=["cst"], writes=["cbf_sel2"])
    S.add("pool", lambda e: e.tensor_copy(out=C.sel2_bf[64:128, :], in_=cs("ident")[64:128, 64:128]), reads=["cst", "cbf_sel2"], writes=["cbf_sel2"])


NQKV = 9216
SCALE = 128 ** -0.5


def attn_sublayer(C, x_in_d, x_out_d, gain, w_qkv_d, w_ao_d, asp, rope_d, out_key, stop_after=None):
    S = C.S
    cst = C.cst
    P = C.ps

    def cc(name, a=None, b=None):
        o0, o1 = CST[name]
        if a is None:
            return cst[:, o0:o1]
        return cst[:, o0 + a:o0 + b]

    C.new_phase()
    hT = C.alloc_bf16(KC * T).rearrange("p (k n) -> p k n", k=KC)
    rope = C.alloc_f32(2 * T)
    S.add("sp", dma(rope, rope_d), writes=["rope"], dma_key="rope")
    mark = C.off
    rmsnorm_phase(C, x_in_d, gain, hT, "mix1")
    S.barrier()
    C.off = mark
    wb = [[C.alloc_bf16(KC * 512).rearrange("p (k n) -> p k n", k=KC) for _ in range(2)]]
    ub = [C.alloc_f32(1024) for _ in range(2)]
    sqb = [C.alloc_f32(1024) for _ in range(2)]
    rsb = [C.alloc_f32(1024) for _ in range(2)]
    t1b = [C.alloc_f32(1024) for _ in range(2)]
    t2b = [C.alloc_f32(1024) for _ in range(2)]
    qko = [C.alloc_bf16(1024) for _ in range(2)]
    vTb = [C.alloc_bf16(1024) for _ in range(2)]
    vsb = [C.alloc_bf16(1024) for _ in range(2)]
    ones = cc("ones")
    p6b = P[6].bitcast(BF16)
    ctr = [0]

    pending = []

    def flush():
        while pending:
            pending.pop(0)()

    def epi(blk, th, prs, mw):
        flush()
        b = ctr[0] % 2
        ctr[0] += 1
        pp = ("pp", prs[0])
        ps = C.ps2(prs[0])
        cols = slice(th * 1024, (th + 1) * 1024)
        g, j, head = blk // 24, (blk // 8) % 3, blk % 8
        if j < 2:
            gcol = cc("qkn", 2 * g + j, 2 * g + j + 1)
            u, sq, rs, t1, t2 = ub[b], sqb[b], rsb[b], t1b[b], t2b[b]
            S.add("act", lambda e: e.activation(out=u, in_=ps, func=AF.Identity, scale=gcol), reads=[pp, "cst"], writes=[("u", b)])
            S.add("act", lambda e: e.activation(out=sq, in_=ps, func=AF.Square), reads=[pp], writes=[("sq", b)])
            pending.append(lambda: epi_qk_rest(b, g, j, head, th, cols, u, sq, rs, t1, t2))
        else:
            S.add("act", lambda e: e.activation(out=vTb[b], in_=ps, func=AF.Copy), reads=[pp], writes=[("vT", b)])
            pending.append(lambda: epi_v_rest(b, g, head, th))

    def epi_qk_rest(b, g, j, head, th, cols, u, sq, rs, t1, t2):
        if True:
            for t2_ in range(2):
                S.add("pe", lambda e, t2_=t2_: e.matmul(P[6 + t2_][:, :], ones, sq[:, t2_ * 512:(t2_ + 1) * 512], start=True, stop=True),
                      reads=[("sq", b), "cst"], writes=[("pp", 3)])
            S.add("act", lambda e: e.activation(out=rs, in_=C.ps2(3), func=AF.Ln, scale=1.0 / 128, bias=C.eps_ap), reads=[("pp", 3)], writes=[("rs", b)])
            S.add("act", lambda e: e.activation(out=rs, in_=rs, func=AF.Exp, scale=-0.5), reads=[("rs", b)], writes=[("rs", b)])
            S.add("dve", lambda e: e.tensor_tensor(out=t1, in0=u, in1=rope[:, cols], op=ALU.mult), reads=[("u", b), "rope"], writes=[("t1", b)])
            S.add("dve", lambda e: e.tensor_tensor(out=t2[0:64, :], in0=u[64:128, :], in1=rope[64:128, T + th * 1024:T + (th + 1) * 1024], op=ALU.mult),
                  reads=[("u", b), "rope"], writes=[("t2a", b)])
            S.add("dve", lambda e: e.tensor_tensor(out=t2[64:128, :], in0=u[0:64, :], in1=rope[0:64, T + th * 1024:T + (th + 1) * 1024], op=ALU.mult),
                  reads=[("u", b), "rope"], writes=[("t2b", b)])
            S.add("dve", lambda e: e.tensor_tensor(out=t1, in0=t1, in1=t2, op=ALU.add), reads=[("t1", b), ("t2a", b), ("t2b", b)], writes=[("t1", b)])
            S.add("dve", lambda e: e.tensor_tensor(out=qko[b], in0=t1, in1=rs, op=ALU.mult), reads=[("t1", b), ("rs", b)], writes=[("qko", b)])
            bi = (g * 2 + j) * 8 + head
            S.add("sp", dma(asp["qk"][bi * 128:(bi + 1) * 128, cols], qko[b]), reads=[("qko", b)], writes=["qkd"], dma_key=("qko", b))

    def epi_v_rest(b, g, head, th):
        if True:
            for i in range(8):
                S.add("pe", lambda e, i=i: e.transpose(p6b[:, i * 128:(i + 1) * 128], vTb[b][:, i * 128:(i + 1) * 128], C.ident_bf),
                      reads=[("vT", b), "cbf_ident"], writes=[("pp", 3)])
            S.add("dve", lambda e: e.tensor_copy(out=vsb[b], in_=p6b), reads=[("pp", 3)], writes=[("vs", b)])
            gh = g * 8 + head
            r0 = gh * 2048 + th * 1024
            S.add("sp", dma(asp["v"][r0:r0 + 1024, :].rearrange("(k p) d -> p k d", p=128), vsb[b].rearrange("p (k d) -> p k d", k=8)),
                  reads=[("vs", b)], writes=["vd"], dma_key=("vs", b))

    gemm(C, [w_qkv_d], KC, hT, [("hT", i) for i in range(4)], NQKV, 512, wb, epi, pairs=(0, 1, 2))
    flush()
    if stop_after == "A":
        return

    C.new_phase()
    mT = C.alloc_bf16(8 * T).rearrange("p (h n) -> p h n", h=8)
    qb_ = [[C.alloc_bf16(T) for _ in range(3)] for _ in range(2)]
    kb_ = [[C.alloc_bf16(T) for _ in range(3)] for _ in range(2)]
    v0b = [C.alloc_bf16(T).rearrange("p (b d) -> p b d", b=16) for _ in range(2)]
    v1b = [C.alloc_bf16(T).rearrange("p (s r d) -> p s r d", s=4, r=4) for _ in range(2)]
    v2b = [C.alloc_bf16(T).rearrange("p (r d) -> p r d", r=16) for _ in range(2)]
    Pt = [C.alloc_bf16(128) for _ in range(4)]
    rden = C.alloc_f32(1024)
    mA = C.maskA4_bf[:, 0:128]
    mB = C.maskB_bf
    tctr = [0]

    def load_head(h):
        hb = h % 2
        for g in range(3):
            bq = (g * 2 + 0) * 8 + h
            bk = (g * 2 + 1) * 8 + h
            S.add("sp", dma(qb_[hb][g], asp["qk"][bq * 128:(bq + 1) * 128, :]), reads=["qkd"], writes=[("q", hb, g)], dma_key=("q", hb, g))
            S.add("sp", dma(kb_[hb][g], asp["qk"][bk * 128:(bk + 1) * 128, :]), reads=["qkd"], writes=[("k", hb, g)], dma_key=("k", hb, g))
        vv = [asp["v"][(g * 8 + h) * 2048:(g * 8 + h + 1) * 2048, :] for g in range(3)]
        S.add("sp", dma(v0b[hb], vv[0].rearrange("(b p) d -> p b d", p=128)), reads=["vd"], writes=[("v", hb, 0)], dma_key=("v", hb, 0))
        for sb in range(4):
            S.add("sp", dma(v1b[hb][:, sb, :, :], vv[1][sb * 512:(sb + 1) * 512, :].rearrange("(i r) d -> i r d", r=4)), reads=["vd"],
                  writes=[("v", hb, 1, sb)], dma_key=("v", hb, 1, sb))
        S.add("sp", dma(v2b[hb], vv[2].rearrange("(i r) d -> i r d", r=16)), reads=["vd"], writes=[("v", hb, 2)], dma_key=("v", hb, 2))

    def tile_score(k_ap, q_ap, nq, mask_ap, rd):
        t = tctr[0] % 4
        tctr[0] += 1
        sbk = 4 + t
        S.add("pe", lambda e: e.matmul(P[sbk][:, 0:nq], k_ap, q_ap, start=True, stop=False), reads=rd, writes=[("sc", t)])
        S.add("pe", lambda e: e.matmul(P[sbk][:, 0:nq], C.ident_bf, mask_ap, start=False, stop=True), reads=["cbf_ident", "cbf_maskA", "cbf_maskB"], writes=[("sc", t)])
        S.add("act", lambda e: e.activation(out=Pt[t][:, 0:nq], in_=P[sbk][:, 0:nq], func=AF.Exp, scale=SCALE), reads=[("sc", t)], writes=[("pt", t)])
        return t

    def tile_pv(t, v_ap, outs, rd, started, last):
        for (bn, bd, csl, psl) in outs:
            st = bn not in started
            started.add(bn)
            S.add("pe", lambda e, bn=bn, csl=csl, psl=psl, st=st: e.matmul(P[bn][:, csl], v_ap, Pt[t][:, psl], start=st, stop=last),
                  reads=rd + [("pt", t)], writes=[("acc", bn)])
            S.add("pe", lambda e, bd=bd, csl=csl, psl=psl, st=st: e.matmul(P[bd][:, csl], C.ones_bf, Pt[t][:, psl], start=st, stop=last),
                  reads=["cbf_ones", ("pt", t)], writes=[("acc", bd)])

    def do_head(h):
        hb = h % 2
        q0, q1, q2 = qb_[hb]
        k0, k1, k2 = kb_[hb]
        rd0 = [("q", hb, 0), ("k", hb, 0), ("v", hb, 0)]
        rd2 = [("q", hb, 2), ("k", hb, 2), ("v", hb, 2)]
        for qh in range(2):
            started = set()
            tiles = []
            for s_ in range(2):
                sb = 2 * qh + s_
                bn, bd = s_, 2 + s_
                for qq in range(4):
                    qb = 4 * sb + qq
                    csl = slice(qq * 128, (qq + 1) * 128)
                    qa = q0[:, qb * 128:(qb + 1) * 128]
                    if qb > 0:
                        tiles.append((k0[:, (qb - 1) * 128:qb * 128], qa, 128, mB, v0b[hb][:, qb - 1, :], [(bn, bd, csl, slice(0, 128))], rd0))
                    tiles.append((k0[:, qb * 128:(qb + 1) * 128], qa, 128, mA, v0b[hb][:, qb, :], [(bn, bd, csl, slice(0, 128))], rd0))
            for s_ in range(2):
                sb = 2 * qh + s_
                bn, bd = s_, 2 + s_
                for r in range(4):
                    csl = slice(r, 512, 4)
                    qa = q1[:, 512 * sb + r:512 * (sb + 1):4]
                    if sb > 0:
                        rd1p = [("q", hb, 1), ("k", hb, 1), ("v", hb, 1, sb - 1)]
                        tiles.append((k1[:, 512 * (sb - 1) + r:512 * sb:4], qa, 128, mB, v1b[hb][:, sb - 1, r, :], [(bn, bd, csl, slice(0, 128))], rd1p))
                    rd1 = [("q", hb, 1), ("k", hb, 1), ("v", hb, 1, sb)]
                    tiles.append((k1[:, 512 * sb + r:512 * (sb + 1):4], qa, 128, mA, v1b[hb][:, sb, r, :], [(bn, bd, csl, slice(0, 128))], rd1))
            for r in range(16):
                qa = q2[:, 1024 * qh + r:1024 * (qh + 1):16]
                outs = [(s_, 2 + s_, slice(r, 512, 16), slice(32 * s_, 32 * s_ + 32)) for s_ in range(2)]
                tiles.append((k2[:, r:T:16], qa, 64, C.maskA4_bf[:, 64 * qh:64 * qh + 64], v2b[hb][:, r, :], outs, rd2))
            LOOK = 2
            slots = {}
            for ti in range(len(tiles) + LOOK):
                if ti < len(tiles):
                    ka, qa, nq, ma, va, outs, rd = tiles[ti]
                    slots[ti] = tile_score(ka, qa, nq, ma, rd)
                tj = ti - LOOK
                if tj >= 0:
                    ka, qa, nq, ma, va, outs, rd = tiles[tj]
                    tile_pv(slots[tj], va, outs, rd, started, tj == len(tiles) - 1)
            S.add("dve", lambda e: e.reciprocal(out=rden, in_=C.ps_all[:, 1024:2048]), reads=[("acc", 2), ("acc", 3)], writes=["rden"])
            S.add("dve", lambda e, qh=qh, h=h: e.tensor_tensor(out=mT[:, h, qh * 1024:(qh + 1) * 1024], in0=C.ps_all[:, 0:1024], in1=rden, op=ALU.mult),
                  reads=[("acc", 0), ("acc", 1), "rden"], writes=[("mT", h)])

    load_head(0)
    for h in range(8):
        if h + 1 < 8:
            load_head(h + 1)
        do_head(h)
    if stop_after == "B":
        return

    S.barrier()
    wob = [[C.alloc_bf16(8 * 512).rearrange("p (k n) -> p k n", k=8) for _ in range(2)]]
    epi3 = resid_epilogue(C, x_in_d, x_out_d, out_key)
    gemm(C, [w_ao_d], 8, mT, [("mT", h) for h in range(8)], D, 512, wob, epi3)


def make_rope():
    half = 64
    inv_freq = 10000.0 ** (-np.arange(half, dtype=np.float64) / half)
    ang = np.arange(T, dtype=np.float64)[None, :] * inv_freq[:, None]
    cos = np.concatenate([np.cos(ang), np.cos(ang)], 0)
    sin = np.concatenate([np.sin(ang), -np.sin(ang)], 0)
    return np.ascontiguousarray(np.concatenate([cos, sin], 1).astype(np.float32))


ENABLE_MAMBA = True
ENABLE_ATTN = True


def build_program():
    nc = bass.Bass("TRN2", target_bir_lowering=False)

    def din(name, shape):
        return nc.dram_tensor(name, shape, F32, kind="ExternalInput").ap()

    def scratch(name, shape, dt=F32):
        return nc.dram_tensor(name, shape, dt, kind="Internal").ap()

    x_d = din("xT", [D, T])
    cst_d = din("cst", [128, NCST])
    w_in = din("ssm_w_in", [D, 10304])
    w_so = din("ssm_w_out", [4096, D])
    w_qkv = din("attn_w_qkv", [D, 9216])
    w_ao = din("attn_w_out", [1024, D])
    wg = [din("wg%d" % l, [D, DFF]) for l in range(2)]
    wu = [din("wu%d" % l, [D, DFF]) for l in range(2)]
    wd = [din("wd%d" % l, [DFF, D]) for l in range(2)]
    out_d = nc.dram_tensor("outT", [D, T], F32, kind="ExternalOutput").ap()
    x1 = scratch("x1", [D, T])
    x2 = scratch("x2", [D, T])
    x3 = scratch("x3", [D, T])
    act_d = scratch("act_sp", [DFF, T], BF16)
    sp = {"zs": [scratch("zs%d" % i, [1024, T]) for i in range(4)], "xs": [scratch("xs%d" % i, [1024, T]) for i in range(4)],
          "bs": scratch("bs", [1024, T]), "cs": scratch("cs", [1024, T]), "yn": scratch("yn", [4096, T], BF16)}
    asp = {"qk": scratch("qk", [48 * 128, T], BF16), "v": scratch("v", [24 * 2048, 128], BF16)}
    rope_d = din("rope", [128, 2 * T])
    es = ExitStack()
    with es:
        C = Ctx(nc, es)
        S = C.S
        setup_consts(C, cst_d)

        def g(name):
            o0, o1 = CST[name]
            return C.cst[:, o0:o1]

        cur = x_d
        if ENABLE_MAMBA:
            mamba_sublayer(C, cur, x1, g("mixn0"), w_in, w_so, sp, "x1")
            cur = x1
        ffn_sublayer(C, cur, x2, g("ffnn0"), wg[0], wu[0], wd[0], act_d, "x2")
        cur = x2
        if ENABLE_ATTN:
            attn_sublayer(C, cur, x3, g("mixn1"), w_qkv, w_ao, asp, rope_d, "x3")
            cur = x3
        ffn_sublayer(C, cur, out_d, g("ffnn1"), wg[1], wu[1], wd[1], act_d, "out")
        S.add("sp", lambda e: e.nop(), reads=[("out", 0), ("out", 1)])
        S.emit(nc, es)
    return nc


def kernel(**inputs):
    inp = {k: np.asarray(v) for k, v in inputs.items()}
    nc = build_program()
    cst = make_cst(inp)
    shared = {
        "cst": cst,
        "rope": make_rope(),
        "ssm_w_in": np.ascontiguousarray(inp["ssm_w_in"][0], dtype=np.float32),
        "ssm_w_out": np.ascontiguousarray(inp["ssm_w_out"][0], dtype=np.float32),
        "attn_w_qkv": np.ascontiguousarray(inp["attn_w_qkv"][0], dtype=np.float32),
        "attn_w_out": np.ascontiguousarray(inp["attn_w_out"][0], dtype=np.float32),
    }
    for l in range(2):
        shared["wg%d" % l] = np.ascontiguousarray(inp["ffn_w_gate"][l], dtype=np.float32)
        shared["wu%d" % l] = np.ascontiguousarray(inp["ffn_w_up"][l], dtype=np.float32)
        shared["wd%d" % l] = np.ascontiguousarray(inp["ffn_w_down"][l], dtype=np.float32)
    x = inp["x"].astype(np.float32)
    in_maps = []
    for b in range(8):
        m = dict(shared)
        m["xT"] = np.ascontiguousarray(x[b].T)
        in_maps.append(m)
    res = run_bass_kernel_spmd(nc, in_maps, core_ids=list(range(8)))
    out = np.stack([np.ascontiguousarray(r["outT"].T) for r in res.results], 0)
    return out.astype(np.float32)
```

```python
import numpy as np
from contextlib import ExitStack
import concourse.bass as bass
import concourse.mybir as mybir
from concourse.bass_utils import run_bass_kernel_spmd

F32 = mybir.dt.float32
BF16 = mybir.dt.bfloat16
AF = mybir.ActivationFunctionType
ALU = mybir.AluOpType
AX = mybir.AxisListType

COMPUTE = ("pe", "act", "dve", "pool")
ALLENG = ("pe", "act", "dve", "pool", "sp")

D = 2048
T = 2048
KC = D // 128
DFF = 5632
FC = DFF // 128
EPS = 1e-6


class _Op:
    __slots__ = ("eng", "fn", "dma", "key", "ndma", "pos", "cnt", "needed", "waits", "vc")


class Sched:
    def __init__(self):
        self.ops = []
        self.lw = {}
        self.rd = {}
        self.pos = {e: 0 for e in ALLENG}
        self.know = {e: {} for e in ALLENG}
        self.dma_cnt = {}

    def add(self, eng, fn, reads=(), writes=(), dma_key=None, ndma=1, phase=True):
        ops = self.ops
        i = len(ops)
        o = _Op()
        o.eng = eng
        o.fn = fn
        o.dma = dma_key is not None
        o.key = dma_key
        o.ndma = ndma
        o.needed = False
        if phase:
            reads = list(reads) + ["PHASE"]
        deps = set()
        lw = self.lw
        rd = self.rd
        for r in reads:
            w = lw.get(r)
            if w is not None:
                deps.add(w)
        for r in writes:
            w = lw.get(r)
            if w is not None:
                deps.add(w)
            d = rd.get(r)
            if d:
                deps.update(d.values())
        for r in reads:
            d = rd.get(r)
            if d is None:
                d = rd[r] = {}
            if o.dma:
                d[("dma", i)] = i
            else:
                d[eng] = i
        for r in writes:
            lw[r] = i
            rd[r] = {}
        know = self.know[eng]
        waits = []
        for p in sorted(deps):
            P = ops[p]
            if P.dma:
                if know.get(("dma", P.key), 0) >= P.cnt:
                    continue
            else:
                if P.eng == "pe" and eng == "pe" and not o.dma:
                    continue
                if know.get(P.eng, 0) >= P.pos:
                    continue
            waits.append(p)
            P.needed = True
            for kk, vv in P.vc.items():
                if know.get(kk, 0) < vv:
                    know[kk] = vv
        o.waits = waits
        if o.dma:
            c = self.dma_cnt.get(dma_key, 0) + 16 * ndma
            self.dma_cnt[dma_key] = c
            o.cnt = c
            vc = dict(know)
            vc[("dma", dma_key)] = c
            o.pos = 0
        else:
            self.pos[eng] += 1
            o.pos = self.pos[eng]
            o.cnt = 0
            vc = dict(know)
            vc[eng] = o.pos
        o.vc = vc
        ops.append(o)
        return i

    def barrier(self):
        self.add("pool", lambda e: e.memset(self.bar_ap, 0.0), reads=(), writes=["PHASE"], phase=False)

    def emit(self, nc, es):
        ops = self.ops
        sem_eng = {e: es.enter_context(nc.semaphore("s_" + e)) for e in COMPUTE}
        sem_dma = {}
        for n, k in enumerate(self.dma_cnt):
            sem_dma[k] = es.enter_context(nc.semaphore("d%d" % n))
        val = {}
        run = {e: 0 for e in COMPUTE}
        for i, o in enumerate(ops):
            if not o.dma and o.needed:
                run[o.eng] += 1
                val[i] = run[o.eng]
        self.sem_max = dict(run)
        per = {e: [] for e in ALLENG}
        for i, o in enumerate(ops):
            per[o.eng].append(i)
        engobj = {"pe": "tensor", "act": "scalar", "dve": "vector", "pool": "gpsimd", "sp": "sync"}

        def body(ename):
            def f(eng):
                for i in per[ename]:
                    o = ops[i]
                    for p in o.waits:
                        P = ops[p]
                        if P.dma:
                            eng.wait_ge(sem_dma[P.key], P.cnt)
                        else:
                            eng.wait_ge(sem_eng[P.eng], val[p])
                    if o.dma:
                        o.fn(eng, sem_dma[o.key])
                    else:
                        ins = o.fn(eng)
                        if o.needed:
                            ins.then_inc(sem_eng[ename], 1)
            return f

        with nc.Block() as block:
            for ename in ("sp", "pool", "act", "dve", "pe"):
                if per[ename]:
                    getattr(block, engobj[ename])(body(ename))


class Ctx:
    ARENA_F32 = 49 * 1024

    def __init__(self, nc, es):
        self.nc = nc
        self.S = Sched()
        self.arena = es.enter_context(nc.sbuf_tensor("arena", [128, self.ARENA_F32], F32))
        self.ps_all = es.enter_context(nc.psum_tensor("ps", [128, 4096], F32))
        self.ps = [self.ps_all[:, i * 512:(i + 1) * 512] for i in range(8)]
        self.off = 0
        self.persist = 0
        self.uid = 0
        self.S.bar_ap = self.alloc_f32(1, persist=True)

    def alloc_f32(self, n, persist=False):
        n = (n + 1) // 2 * 2
        a = self.arena[:, self.off:self.off + n]
        self.off += n
        assert self.off <= self.ARENA_F32, "SBUF arena overflow %d" % self.off
        if persist:
            assert self.persist == self.off - n
            self.persist = self.off
        return a

    def alloc_bf16(self, n, persist=False):
        return self.alloc_f32((n + 1) // 2, persist).bitcast(BF16)[:, 0:n]

    def new_phase(self):
        self.S.barrier()
        self.off = self.persist

    def ps2(self, pr):
        return self.ps_all[:, pr * 1024:(pr + 1) * 1024]

    def name(self, s):
        self.uid += 1
        return "%s#%d" % (s, self.uid)


def dma(out, in_, **kw):
    return lambda e, s: e.dma_start(out=out, in_=in_, **kw).then_inc(s, 16)


def rmsnorm_phase(C, x_d, gain, hT, tag, tile=512):
    S = C.S
    xs = [C.alloc_f32(KC * tile).rearrange("p (k n) -> p k n", k=KC) for _ in range(2)]
    sq = C.alloc_f32(KC * tile).rearrange("p (k n) -> p k n", k=KC)
    rstd = [C.alloc_f32(tile) for _ in range(2)]
    ones = C.ones_f32
    for tt in range(T // tile):
        b = tt % 2
        xsb = xs[b]
        hres = ("hT", (tt * tile) // 512)
        S.add("sp", dma(xsb, x_d[:, tt * tile:(tt + 1) * tile].rearrange("(k p) n -> p k n", p=128)),
              writes=[("xs", b)], dma_key=("xs", b))
        S.add("act", lambda e, xsb=xsb: e.activation(out=sq, in_=xsb, func=AF.Square), reads=[("xs", b)], writes=["sq"])
        for k in range(KC):
            S.add("pe", lambda e, k=k: e.matmul(C.ps[7][:, 0:tile], ones, sq[:, k, :], start=(k == 0), stop=(k == KC - 1)), reads=["sq"], writes=["ps7"])
        S.add("act", lambda e, b=b: e.activation(out=rstd[b], in_=C.ps[7][:, 0:tile], func=AF.Sqrt, scale=1.0 / D, bias=C.eps_ap),
              reads=["ps7"], writes=[("rstd", b)])
        S.add("dve", lambda e, b=b: e.reciprocal(out=rstd[b], in_=rstd[b]), reads=[("rstd", b)], writes=[("rstd", b)])
        for k in range(KC):
            S.add("dve", lambda e, k=k, b=b, xsb=xsb, tt=tt: e.scalar_tensor_tensor(
                out=hT[:, k, tt * tile:(tt + 1) * tile], in0=xsb[:, k, :], scalar=gain[:, k:k + 1], in1=rstd[b],
                op0=ALU.mult, op1=ALU.mult), reads=[("xs", b), ("rstd", b)], writes=[hres])


def gemm(C, W_list, kch, actT, act_reads, ncols_total, gcols, wbufs, epilogue, th_list=(0, 1), tcols=1024, pairs=(0, 1, 2, 3)):
    S = C.S
    nW = len(W_list)
    ngroups = (ncols_total + gcols - 1) // gcols
    nbuf = len(wbufs[0])
    pp_ctr = [0]

    def load(g):
        c0 = g * gcols
        nc_ = min(gcols, ncols_total - c0)
        for wi in range(nW):
            slot = g % nbuf
            S.add("pool", dma(wbufs[wi][slot][:, :, 0:nc_], W_list[wi][:, c0:c0 + nc_].rearrange("(k p) n -> p k n", p=128)),
                  writes=[("w", wi, slot)], dma_key=("w", wi, slot))

    for g in range(min(nbuf - 1, ngroups)):
        load(g)
    for g in range(ngroups):
        if g + nbuf - 1 < ngroups:
            load(g + nbuf - 1)
        c0 = g * gcols
        nc_ = min(gcols, ncols_total - c0)
        slot = g % nbuf
        for mb in range((nc_ + 127) // 128):
            mw = min(128, nc_ - mb * 128)
            blk = (c0 + mb * 128) // 128
            for th in th_list:
                prs = []
                for wi in range(nW):
                    pr = pairs[pp_ctr[0] % len(pairs)]
                    pp_ctr[0] += 1
                    prs.append(pr)
                    wb = wbufs[wi][slot]
                    for k in range(kch):
                        for t2 in range(tcols // 512):
                            S.add("pe", lambda e, pr=pr, wb=wb, k=k, t2=t2, mb=mb, mw=mw, th=th: e.matmul(
                                C.ps[2 * pr + t2][0:mw, :], wb[:, k, mb * 128:mb * 128 + mw],
                                actT[:, k, th * tcols + t2 * 512: th * tcols + (t2 + 1) * 512],
                                start=(k == 0), stop=(k == kch - 1)),
                                reads=[("w", wi, slot)] + act_reads, writes=[("pp", pr)])
                epilogue(blk, th, prs, mw)


def ffn_sublayer(C, x_in_d, x_out_d, gain, wg_d, wu_d, wd_d, act_d, out_key):
    S = C.S
    C.new_phase()
    hT = C.alloc_bf16(KC * T).rearrange("p (k n) -> p k n", k=KC)
    mark = C.off
    rmsnorm_phase(C, x_in_d, gain, hT, "ffn")
    S.barrier()
    C.off = mark
    wb = [[C.alloc_bf16(KC * 512).rearrange("p (k n) -> p k n", k=KC) for _ in range(3)] for _ in range(2)]
    sg = [C.alloc_f32(1024) for _ in range(2)]
    ao = [C.alloc_bf16(1024) for _ in range(2)]
    ectr = [0]

    def epi(blk, th, prs, mw):
        b = ectr[0] % 2
        ectr[0] += 1
        g_ps = C.ps2(prs[0])
        u_ps = C.ps2(prs[1])
        S.add("act", lambda e: e.activation(out=sg[b], in_=g_ps, func=AF.Silu), reads=[("pp", prs[0])], writes=[("sg", b)])
        S.add("dve", lambda e: e.tensor_tensor(out=ao[b], in0=sg[b], in1=u_ps, op=ALU.mult),
              reads=[("sg", b), ("pp", prs[1])], writes=[("ao", b)])
        S.add("sp", dma(act_d[blk * 128:(blk + 1) * 128, th * 1024:(th + 1) * 1024], ao[b]),
              reads=[("ao", b)], writes=[("actd", th)], dma_key=("ao", b))

    gemm(C, [wg_d, wu_d], KC, hT, [("hT", i) for i in range(4)], DFF, 512, wb, epi)
    C.new_phase()
    aT = C.alloc_bf16(FC * 1024).rearrange("p (k n) -> p k n", k=FC)
    wdb = [[C.alloc_bf16(FC * 256).rearrange("p (k n) -> p k n", k=FC) for _ in range(3)]]
    xr = [C.alloc_f32(1024) for _ in range(2)]
    xo = [C.alloc_f32(1024) for _ in range(2)]
    for th in range(2):
        for q in range(4):
            k0, k1 = q * FC // 4, (q + 1) * FC // 4
            S.add("sp", dma(aT[:, k0:k1, :], act_d[k0 * 128:k1 * 128, th * 1024:(th + 1) * 1024].rearrange("(k p) n -> p k n", p=128)),
                  reads=[("actd", th)], writes=[("aT", q)], dma_key=("aT", q))
        ectr2 = [0]

        def epi2(blk, th_, prs, mw, th=th):
            b = ectr2[0] % 2
            ectr2[0] += 1
            S.add("act", dma(xr[b], x_in_d[blk * 128:(blk + 1) * 128, th * 1024:(th + 1) * 1024]),
                  writes=[("xr", b)], dma_key=("xr", b))
            S.add("dve", lambda e: e.tensor_tensor(out=xo[b], in0=xr[b], in1=C.ps2(prs[0]), op=ALU.add),
                  reads=[("xr", b), ("pp", prs[0])], writes=[("xo", b)])
            S.add("sp", dma(x_out_d[blk * 128:(blk + 1) * 128, th * 1024:(th + 1) * 1024], xo[b]),
                  reads=[("xo", b)], writes=[(out_key, b)], dma_key=("xo", b))

        gemm(C, [wd_d], FC, aT, [("aT", q) for q in range(4)], D, 256, wdb, epi2, th_list=(0,))


CST = {}
_o = 0
for _n, _w in (("mixn0", 16), ("ffnn0", 16), ("mixn1", 16), ("ffnn1", 16), ("convw", 192), ("convb", 48),
               ("dtb", 1), ("alog", 1), ("dskip", 64), ("ssmn", 32), ("ident", 128), ("tri", 128),
               ("qkn", 6), ("maskA", 128), ("maskB", 128), ("ones", 128)):
    CST[_n] = (_o, _o + _w)
    _o += _w
NCST = _o

SSM_DI = 4096
NH = 64


def resid_epilogue(C, x_in_d, x_out_d, out_key, th_of=None):
    S = C.S
    xr = [C.alloc_f32(1024) for _ in range(2)]
    xo = [C.alloc_f32(1024) for _ in range(2)]
    ctr = [0]

    def epi(blk, th, prs, mw):
        if th_of is not None:
            th = th_of[0]
        b = ctr[0] % 2
        ctr[0] += 1
        S.add("act", dma(xr[b], x_in_d[blk * 128:(blk + 1) * 128, th * 1024:(th + 1) * 1024]),
              writes=[("xr", b)], dma_key=("xr", b))
        S.add("dve", lambda e: e.tensor_tensor(out=xo[b], in0=xr[b], in1=C.ps2(prs[0]), op=ALU.add),
              reads=[("xr", b), ("pp", prs[0])], writes=[("xo", b)])
        S.add("sp", dma(x_out_d[blk * 128:(blk + 1) * 128, th * 1024:(th + 1) * 1024], xo[b]),
              reads=[("xo", b)], writes=[(out_key, b)], dma_key=("xo", b))
    return epi


def mamba_sublayer(C, x_in_d, x_out_d, gain, w_in_d, w_out_d, sp, out_key, stop_after=None):
    S = C.S
    cst = C.cst

    def cc(name, a=None, b=None):
        o0, o1 = CST[name]
        if a is None:
            return cst[:, o0:o1]
        return cst[:, o0 + a:o0 + b]

    C.new_phase()
    dtT = C.alloc_f32(T)
    aT = C.alloc_f32(T)
    aneg = C.alloc_f32(2)
    keep = C.off
    hT = C.alloc_bf16(KC * T).rearrange("p (k n) -> p k n", k=KC)
    mark = C.off
    rmsnorm_phase(C, x_in_d, gain, hT, "mix0")
    S.barrier()
    C.off = mark
    wb = [[C.alloc_bf16(KC * 512).rearrange("p (k n) -> p k n", k=KC) for _ in range(3)]]
    ub = [C.alloc_f32(1028) for _ in range(2)]
    acc = [C.alloc_f32(1024) for _ in range(2)]
    so = [C.alloc_f32(1024) for _ in range(2)]
    S.add("act", lambda e: e.activation(out=aneg[0:64, 0:1], in_=cc("alog")[0:64, :], func=AF.Exp), reads=["cst"], writes=["aneg"])
    S.add("dve", lambda e: e.tensor_scalar(out=aneg[0:64, 0:1], in0=aneg[0:64, 0:1], scalar1=-1.0, scalar2=None, op0=ALU.mult),
          reads=["aneg"], writes=["aneg"])
    ctr = [0]

    def epi(blk, th, prs, mw):
        b = ctr[0] % 2
        ctr[0] += 1
        pp = ("pp", prs[0])
        ps = C.ps2(prs[0])
        cols = slice(th * 1024, (th + 1) * 1024)
        if blk < 32:
            S.add("act", lambda e: e.activation(out=so[b], in_=ps, func=AF.Silu), reads=[pp], writes=[("so", b)])
            S.add("sp", dma(sp["zs"][blk // 8][(blk % 8) * 128:(blk % 8 + 1) * 128, cols], so[b]), reads=[("so", b)], writes=["zs"], dma_key=("so", b))
        elif blk < 80:
            j = blk - 32
            if j < 32:
                dst = sp["xs"][j // 8][(j % 8) * 128:(j % 8 + 1) * 128, cols]
                dkey = "xs"
            elif j < 40:
                dst = sp["bs"][(j - 32) * 128:(j - 31) * 128, cols]
                dkey = "bs"
            else:
                dst = sp["cs"][(j - 40) * 128:(j - 39) * 128, cols]
                dkey = "cs"
            u = ub[b]
            S.add("act", lambda e: e.activation(out=u[:, 3:1027], in_=ps, func=AF.Copy), reads=[pp], writes=[("ub", b)])
            if th == 0:
                S.add("pool", lambda e: e.memset(u[:, 0:3], 0.0), writes=[("ubh", b)])
            else:
                S.add("pool", lambda e: e.tensor_copy(out=u[:, 0:3], in_=ub[1 - b][:, 1024:1027]), reads=[("ub", 1 - b)], writes=[("ubh", b)])
            S.add("act", lambda e: e.activation(out=acc[b], in_=ps, func=AF.Identity, scale=cc("convw", 4 * j + 3, 4 * j + 4),
                                                bias=cc("convb", j, j + 1)), reads=[pp, "cst"], writes=[("acc", b)])
            for k in (2, 1, 0):
                S.add("dve", lambda e, k=k: e.scalar_tensor_tensor(out=acc[b], in0=u[:, k:k + 1024], scalar=cc("convw", 4 * j + k, 4 * j + k + 1),
                                                                  in1=acc[b], op0=ALU.mult, op1=ALU.add),
                      reads=[("ub", b), ("ubh", b), ("acc", b)], writes=[("acc", b)])
            S.add("act", lambda e: e.activation(out=so[b], in_=acc[b], func=AF.Silu), reads=[("acc", b)], writes=[("so", b)])
            S.add("sp", dma(dst, so[b]), reads=[("so", b)], writes=[dkey], dma_key=("so", b))
        else:
            S.add("act", lambda e: e.activation(out=so[b][0:64, :], in_=ps[0:64, :], func=AF.Exp, bias=cc("dtb")[0:64, :]),
                  reads=[pp, "cst"], writes=[("so", b)])
            S.add("act", lambda e: e.activation(out=dtT[0:64, cols], in_=so[b][0:64, :], func=AF.Ln, bias=cc("ones", 0, 1)[0:64, :]),
                  reads=[("so", b), "cst"], writes=["dtT"])
            S.add("dve", lambda e: e.tensor_scalar(out=aT[0:64, cols], in0=dtT[0:64, cols], scalar1=aneg[0:64, 0:1], scalar2=None, op0=ALU.mult),
                  reads=["dtT", "aneg"], writes=["aT"])

    import os
    if not os.environ.get("MAMBA_SKIPA"):
        gemm(C, [w_in_d], KC, hT, [("hT", i) for i in range(4)], 10304, 512, wb, epi)

    if stop_after == "A":
        return
    S.barrier()
    C.off = keep
    state = C.alloc_f32(SSM_DI)
    state_bf = C.alloc_bf16(SSM_DI)
    xc = [C.alloc_f32(32 * 128).rearrange("p (k n) -> p k n", k=32) for _ in range(2)]
    zcs = [C.alloc_f32(32 * 128).rearrange("p (k n) -> p k n", k=32) for _ in range(2)]
    bc = [C.alloc_f32(8 * 128).rearrange("p (k n) -> p k n", k=8) for _ in range(2)]
    cc_ = [C.alloc_f32(8 * 128).rearrange("p (k n) -> p k n", k=8) for _ in range(2)]
    bbfs = [C.alloc_bf16(8 * 128).rearrange("p (k n) -> p k n", k=8) for _ in range(2)]
    cbfs = [C.alloc_bf16(8 * 128).rearrange("p (k n) -> p k n", k=8) for _ in range(2)]
    ynb = [C.alloc_bf16(32 * 128).rearrange("p (k n) -> p k n", k=32) for _ in range(2)]
    acTs = [C.alloc_f32(128) for _ in range(2)]
    ac2s = [C.alloc_bf16(128) for _ in range(2)]
    acr = C.alloc_f32(128)
    sq1 = [C.alloc_f32(128) for _ in range(2)]
    dteT = C.alloc_f32(128)
    acl = C.alloc_f32(2)
    ac_tm = C.alloc_f32(64)
    nac_tms = [C.alloc_f32(64) for _ in range(2)]
    dt_tms = [C.alloc_f32(64) for _ in range(2)]
    wst_tms = [C.alloc_f32(64) for _ in range(2)]
    eac_tms = [C.alloc_f32(64) for _ in range(2)]
    cdd = C.alloc_f32(64)
    cd_bs = [C.alloc_f32(64) for _ in range(2)]
    xdt = [C.alloc_bf16(512) for _ in range(2)]
    xw = [C.alloc_bf16(512) for _ in range(2)]
    xD = [C.alloc_f32(512) for _ in range(2)]
    btm = [C.alloc_bf16(128) for _ in range(2)]
    cbm = [C.alloc_f32(128) for _ in range(2)]
    E = [C.alloc_f32(512) for _ in range(2)]
    MT = [C.alloc_bf16(1024).rearrange("p (h l) -> p h l", h=8) for _ in range(2)]
    yt = [C.alloc_f32(512) for _ in range(2)]
    ygT = [C.alloc_f32(512) for _ in range(2)]
    sqb = [C.alloc_f32(512) for _ in range(2)]
    rs = [C.alloc_f32(128) for _ in range(2)]
    ident = cc("ident")
    ones = cc("ones")
    P = C.ps
    import os
    NCH = int(os.environ.get('MAMBA_NCH', T // 128))
    DBG = float(os.environ.get('MAMBA_DBG', '99'))

    def load_chunk(c):
        b2 = c % 2
        tcols = slice(c * 128, (c + 1) * 128)
        def ld4(dst, src):
            def f(e, s_):
                for q4 in range(4):
                    e.dma_start(out=dst[:, 8 * q4:8 * q4 + 8, :], in_=src[q4 * 1024:(q4 + 1) * 1024, tcols].rearrange("(k p) n -> p k n", p=128)).then_inc(s_, 16)
            return f
        for q4 in range(4):
            S.add("sp", dma(xc[b2][:, 8 * q4:8 * q4 + 8, :], sp["xs"][q4][:, tcols].rearrange("(k p) n -> p k n", p=128)),
                  reads=["xs"], writes=[("xc", b2, q4)], dma_key=("xc", b2, q4))
        S.add("sp", dma(bc[b2], sp["bs"][:, tcols].rearrange("(k p) n -> p k n", p=128)), reads=["bs"], writes=[("bc", b2)], dma_key=("bc", b2))
        S.add("sp", dma(cc_[b2], sp["cs"][:, tcols].rearrange("(k p) n -> p k n", p=128)), reads=["cs"], writes=[("cc", b2)], dma_key=("cc", b2))
        for q4 in range(4):
            S.add("sp", dma(zcs[b2][:, 8 * q4:8 * q4 + 8, :], sp["zs"][q4][:, tcols].rearrange("(k p) n -> p k n", p=128)),
                  reads=["zs"], writes=[("zc", b2, q4)], dma_key=("zc", b2, q4))

    load_chunk(0)
    def do_chunk(c):
        b2 = c % 2
        tcols = slice(c * 128, (c + 1) * 128)
        zc = zcs[b2]
        bbf = bbfs[b2]
        cbf = cbfs[b2]
        acT = acTs[b2]
        nac_tm = nac_tms[b2]
        dt_tm = dt_tms[b2]
        wst_tm = wst_tms[b2]
        eac_tm = eac_tms[b2]
        cd_b = cd_bs[b2]
        S.add("dve", lambda e, tcols=tcols: e.tensor_tensor_scan(out=acT[0:64, :], data0=ones[0:64, :], data1=aT[0:64, tcols], initial=0.0,
                                                                 op0=ALU.mult, op1=ALU.add), reads=["aT", "cst"], writes=[("acT", b2)])
        ac2 = ac2s[b2]
        S.add("dve", lambda e: e.tensor_copy(out=ac2[0:64, :], in_=acT[0:64, :]), reads=[("acT", b2)], writes=[("ac3", b2)])
        S.add("dve", lambda e: e.tensor_tensor(out=acr[0:64, :], in0=acT[0:64, :], in1=ac2[0:64, :], op=ALU.subtract), reads=[("acT", b2), ("ac3", b2)], writes=["acr"])
        S.add("dve", lambda e: e.tensor_copy(out=ac2[64:128, :], in_=acr[0:64, :]), reads=["acr"], writes=[("ac3", b2)])
        S.add("dve", lambda e: e.tensor_scalar(out=acl[0:64, 0:1], in0=acT[0:64, 127:128], scalar1=1.0, scalar2=None, op0=ALU.mult),
              reads=[("acT", b2)], writes=["aclast"])
        S.add("act", lambda e: e.activation(out=dteT[0:64, :], in_=acT[0:64, :], func=AF.Exp, scale=-1.0, bias=acl[0:64, 0:1]),
              reads=[("acT", b2), "aclast"], writes=["dteT"])
        S.add("pe", lambda e: e.transpose(P[0][:, 0:64], acT[0:64, :], ident[0:64, 0:64]), reads=[("acT", b2), "cst"], writes=["p0"])
        S.add("pe", lambda e, tcols=tcols: e.transpose(P[0][:, 64:128], dtT[0:64, tcols], ident[0:64, 0:64]), reads=["dtT", "cst"], writes=["p0"])
        S.add("pe", lambda e: e.transpose(P[0][:, 128:192], dteT[0:64, :], ident[0:64, 0:64]), reads=["dteT", "cst"], writes=["p0"])
        S.add("dve", lambda e: e.tensor_scalar(out=cdd[0:64, :], in0=ident[0:64, 0:64], scalar1=acl[0:64, 0:1], scalar2=None, op0=ALU.mult),
              reads=["aclast", "cst"], writes=["cdd"])
        S.add("pe", lambda e: e.matmul(P[0][:, 192:256], ones[0:64, :], cdd[0:64, :], start=True, stop=True), reads=["cdd", "cst"], writes=["p0"])
        S.add("act", lambda e: e.activation(out=ac_tm, in_=P[0][:, 0:64], func=AF.Copy), reads=["p0"], writes=["ac_tm"])
        S.add("act", lambda e: e.activation(out=nac_tm, in_=P[0][:, 0:64], func=AF.Identity, scale=-1.0), reads=["p0"], writes=[("nac_tm", b2)])
        S.add("act", lambda e: e.activation(out=eac_tm, in_=P[0][:, 0:64], func=AF.Exp), reads=["p0"], writes=[("eac_tm", b2)])
        S.add("act", lambda e: e.activation(out=dt_tm, in_=P[0][:, 64:128], func=AF.Copy), reads=["p0"], writes=[("dt_tm", b2)])
        S.add("dve", lambda e: e.tensor_tensor(out=wst_tm, in0=dt_tm, in1=P[0][:, 128:192], op=ALU.mult), reads=[("dt_tm", b2), "p0"], writes=[("wst_tm", b2)])
        S.add("act", lambda e: e.activation(out=cd_b, in_=P[0][:, 192:256], func=AF.Exp), reads=["p0"], writes=[("cd_b", b2)])
        S.add("pool", lambda e, b2=b2: e.tensor_copy(out=bbf, in_=bc[b2]), reads=[("bc", b2)], writes=[("bbf", b2)])
        S.add("pool", lambda e, b2=b2: e.tensor_copy(out=cbf, in_=cc_[b2]), reads=[("cc", b2)], writes=[("cbf", b2)])
        def s1(g):
            q = g % 2
            hs = slice(8 * g, 8 * g + 8)
            bx = 1 + q
            for i in range(4):
                S.add("pe", lambda e, i=i, g=g, bx=bx, b2=b2: e.transpose(P[bx][:, i * 128:(i + 1) * 128], xc[b2][:, 4 * g + i, :], ident),
                      reads=[("xc", b2, g // 2), "cst"], writes=[("px", q)])
            x3 = P[bx].rearrange("p (h d) -> p h d", h=8)
            S.add("dve", lambda e, q=q, x3=x3, hs=hs: e.tensor_tensor(out=xdt[q].rearrange("p (h d) -> p h d", h=8), in0=x3,
                                                                     in1=dt_tm[:, hs].unsqueeze(2).broadcast_to([128, 8, 64]), op=ALU.mult),
                  reads=[("px", q), ("dt_tm", b2)], writes=[("xdt", q)])
            S.add("dve", lambda e, q=q, x3=x3, hs=hs: e.tensor_tensor(out=xw[q].rearrange("p (h d) -> p h d", h=8), in0=x3,
                                                                     in1=wst_tm[:, hs].unsqueeze(2).broadcast_to([128, 8, 64]), op=ALU.mult),
                  reads=[("px", q), ("wst_tm", b2)], writes=[("xw", q)])
            S.add("dve", lambda e, q=q, x3=x3, hs=hs: e.tensor_tensor(out=xD[q].rearrange("p (h d) -> p h d", h=8), in0=x3,
                                                                     in1=cc("dskip")[:, hs].unsqueeze(2).broadcast_to([128, 8, 64]), op=ALU.mult),
                  reads=[("px", q), "cst"], writes=[("xD", q)])
            S.add("pe", lambda e, g=g, b2=b2: e.transpose(P[3][:, 0:128], bc[b2][:, g, :], ident), reads=[("bc", b2), "cst"], writes=["p3"])
            S.add("act", lambda e, q=q: e.activation(out=btm[q], in_=P[3][:, 0:128], func=AF.Copy), reads=["p3"], writes=[("btm", q)])
            S.add("pe", lambda e, g=g: e.matmul(P[0][:, 256:384], bbf[:, g, :], cbf[:, g, :], start=True, stop=True), reads=[("bbf", b2), ("cbf", b2)], writes=["p0"])
            S.add("dve", lambda e, q=q: e.tensor_tensor(out=cbm[q], in0=P[0][:, 256:384], in1=cc("tri"), op=ALU.mult), reads=["p0", "cst"], writes=[("cbm", q)])
            for hh in range(2):
                r = hh
                br = 4 + r
                S.add("pe", lambda e, br=br: e.matmul(P[br][:, :], C.ident_bf, C.maskA4_bf, start=True, stop=False),
                      reads=["cbf_ident", "cbf_maskA"], writes=[("pr", r)])
                for i in range(4):
                    h = 8 * g + 4 * hh + i
                    S.add("pe", lambda e, h=h, i=i, br=br: e.matmul(P[br][:, i * 128:(i + 1) * 128], C.sel2_bf[:, h:h + 1].broadcast_to([128, 128]),
                                                                     ac2, start=False, stop=(i == 3)),
                          reads=[("ac3", b2), "cbf_sel2"], writes=[("pr", r)])
                for i in range(4):
                    h = 8 * g + 4 * hh + i
                    S.add("act", lambda e, h=h, i=i, br=br, r=r: e.activation(out=E[r][:, i * 128:(i + 1) * 128], in_=P[br][:, i * 128:(i + 1) * 128],
                                                                            func=AF.Exp, bias=nac_tm[:, h:h + 1]), reads=[("pr", r), ("nac_tm", b2)], writes=[("E", r)])
                S.add("pool", lambda e, q=q, hh=hh, r=r: e.tensor_tensor(
                    out=MT[q][:, 4 * hh:4 * hh + 4, :], in0=E[r].rearrange("p (h l) -> p h l", h=4),
                    in1=cbm[q].unsqueeze(1).broadcast_to([128, 4, 128]), op=ALU.mult),
                    reads=[("E", r), ("cbm", q)], writes=[("MT", q)])
        def s2(g):
            q = g % 2
            hs = slice(8 * g, 8 * g + 8)
            bx = 1 + q
            for i in range(8):
                S.add("pe", lambda e, i=i, q=q: e.matmul(P[6][:, i * 64:(i + 1) * 64], MT[q][:, i, :], xdt[q][:, i * 64:(i + 1) * 64], start=True, stop=True),
                      reads=[("MT", q), ("xdt", q)], writes=["p6"])
            if c > 0:
                S.add("pe", lambda e, g=g: e.matmul(P[7][:, :], cbf[:, g, :], state_bf[:, g * 512:(g + 1) * 512], start=True, stop=True),
                      reads=[("cbf", b2), ("sbf", g)], writes=["p7"])
                S.add("dve", lambda e, q=q, hs=hs: e.tensor_tensor(out=yt[q].rearrange("p (h d) -> p h d", h=8), in0=P[7].rearrange("p (h d) -> p h d", h=8),
                                                                  in1=eac_tm[:, hs].unsqueeze(2).broadcast_to([128, 8, 64]), op=ALU.mult),
                      reads=["p7", ("eac_tm", b2)], writes=[("yt", q)])
                S.add("dve", lambda e, q=q: e.tensor_tensor(out=yt[q], in0=yt[q], in1=P[6][:, :], op=ALU.add), reads=[("yt", q), "p6"], writes=[("yt", q)])
                S.add("pool", lambda e, q=q: e.tensor_tensor(out=yt[q], in0=yt[q], in1=xD[q], op=ALU.add), reads=[("yt", q), ("xD", q)], writes=[("yt", q)])
            else:
                S.add("dve", lambda e, q=q: e.tensor_tensor(out=yt[q], in0=xD[q], in1=P[6][:, :], op=ALU.add), reads=[("xD", q), "p6"], writes=[("yt", q)])
            if c < NCH - 1:
                S.add("pe", lambda e, q=q: e.matmul(P[7][:, :], btm[q], xw[q], start=True, stop=True), reads=[("btm", q), ("xw", q)], writes=["p7"])
                sg_ = state[:, g * 512:(g + 1) * 512]
                if c > 0:
                    S.add("pool", lambda e, sg_=sg_, hs=hs: e.tensor_tensor(out=sg_.rearrange("p (h d) -> p h d", h=8), in0=sg_.rearrange("p (h d) -> p h d", h=8),
                                                                          in1=cd_b[:, hs].unsqueeze(2).broadcast_to([128, 8, 64]), op=ALU.mult),
                          reads=[("st", g), ("cd_b", b2)], writes=[("st", g)])
                    S.add("dve", lambda e, sg_=sg_: e.tensor_tensor(out=sg_, in0=sg_, in1=P[7][:, :], op=ALU.add), reads=[("st", g), "p7"], writes=[("st", g)])
                else:
                    S.add("dve", lambda e, sg_=sg_: e.tensor_copy(out=sg_, in_=P[7][:, :]), reads=["p7"], writes=[("st", g)])
                S.add("act", lambda e, sg_=sg_, g=g: e.activation(out=state_bf[:, g * 512:(g + 1) * 512], in_=sg_, func=AF.Copy),
                      reads=[("st", g)], writes=[("sbf", g)])
        def s3(g):
            q = g % 2
            bx = 1 + q
            for i in range(4):
                S.add("pe", lambda e, i=i, q=q, bx=bx: e.transpose(P[bx][:, i * 128:(i + 1) * 128], yt[q][:, i * 128:(i + 1) * 128], ident),
                      reads=[("yt", q), "cst"], writes=[("px", q)])
            S.add("dve", lambda e, q=q, g=g, bx=bx: e.tensor_tensor(out=ygT[q].rearrange("p (k n) -> p k n", k=4), in0=P[bx].rearrange("p (k n) -> p k n", k=4),
                                                                   in1=zc[:, 4 * g:4 * g + 4, :], op=ALU.mult), reads=[("px", q), ("zc", b2, g // 2)], writes=[("ygT", q)])
            S.add("act", lambda e, q=q: e.activation(out=sqb[q], in_=ygT[q], func=AF.Square), reads=[("ygT", q)], writes=[("sqb", q)])

        def s4(g):
            q = g % 2
            S.add("dve", lambda e, q=q: e.tensor_reduce(out=sq1[q], in_=sqb[q].rearrange("p (k n) -> p n k", k=4), axis=AX.X, op=ALU.add),
                  reads=[("sqb", q)], writes=[("sq1", q)])
            S.add("pe", lambda e, q=q: e.matmul(P[3][:, 384:512], ones, sq1[q], start=True, stop=True), reads=[("sq1", q), "cst"], writes=["p3"])
            S.add("act", lambda e, q=q: e.activation(out=rs[q], in_=P[3][:, 384:512], func=AF.Ln, scale=1.0 / 512, bias=C.eps_ap), reads=["p3"], writes=[("rs", q)])
            S.add("act", lambda e, q=q: e.activation(out=rs[q], in_=rs[q], func=AF.Exp, scale=-0.5), reads=[("rs", q)], writes=[("rs", q)])
            for i in range(4):
                S.add("dve", lambda e, i=i, q=q, g=g, b2=b2: e.scalar_tensor_tensor(
                    out=ynb[b2][:, 4 * g + i, :], in0=ygT[q][:, i * 128:(i + 1) * 128], scalar=cc("ssmn", 4 * g + i, 4 * g + i + 1), in1=rs[q],
                    op0=ALU.mult, op1=ALU.mult), reads=[("ygT", q), ("rs", q), "cst"], writes=[("ynb", b2)])
        def store():
            S.add("sp", dma(sp["yn"][:, tcols].rearrange("(k p) n -> p k n", p=128), ynb[b2]), reads=[("ynb", b2)], writes=["yn"], dma_key=("ynb", b2))
        return s1, s2, s3, s4, store

    if NCH > 1:
        load_chunk(1)
    st = {0: do_chunk(0)}
    seq = [(c, g) for c in range(NCH) for g in range(8)]
    st[0][0](0)
    for idx, (c, g) in enumerate(seq):
        if g == 3 and c + 1 < NCH:
            st[c + 1] = do_chunk(c + 1)
        st[c][1](g)
        if idx + 1 < len(seq):
            c1, g1 = seq[idx + 1]
            st[c1][0](g1)
        st[c][2](g)
        if g == 7 and c + 2 < NCH:
            load_chunk(c + 2)
        if idx > 0:
            c0, g0 = seq[idx - 1]
            st[c0][3](g0)
            if g0 == 7:
                st[c0][4]()
    cl, gl = seq[-1]
    st[cl][3](gl)
    st[cl][4]()

    if stop_after == "B":
        return
    C.new_phase()
    yT = C.alloc_bf16(32 * T).rearrange("p (k n) -> p k n", k=32)
    wob = [[C.alloc_bf16(32 * 256).rearrange("p (k n) -> p k n", k=32) for _ in range(2)]]
    for q4 in range(4):
        S.add("sp", dma(yT[:, 8 * q4:8 * q4 + 8, :], sp["yn"][q4 * 1024:(q4 + 1) * 1024, :].rearrange("(k p) n -> p k n", p=128)),
              reads=["yn"], writes=[("yT", q4)], dma_key=("yT", q4))
    epi3 = resid_epilogue(C, x_in_d, x_out_d, out_key)
    gemm(C, [w_out_d], 32, yT, [("yT", q4) for q4 in range(4)], D, 256, wob, epi3)


def _cols(v, n):
    return np.ascontiguousarray(np.asarray(v, np.float32).reshape(n, 128).T)


def make_cst(inp):
    c = np.zeros((128, NCST), np.float32)

    def put(name, arr):
        o0, o1 = CST[name]
        arr = np.asarray(arr, np.float32)
        c[0:arr.shape[0], o0:o0 + arr.shape[1]] = arr

    put("mixn0", _cols(inp["mix_norm"][0], 16))
    put("ffnn0", _cols(inp["ffn_norm"][0], 16))
    put("mixn1", _cols(inp["mix_norm"][1], 16))
    put("ffnn1", _cols(inp["ffn_norm"][1], 16))
    cw = np.asarray(inp["ssm_conv_w"][0], np.float32)
    put("convw", cw.reshape(4, 48, 128).transpose(2, 1, 0).reshape(128, 192))
    put("convb", _cols(inp["ssm_conv_b"][0], 48))
    put("dtb", np.asarray(inp["ssm_dt_bias"][0], np.float32).reshape(64, 1))
    put("alog", np.asarray(inp["ssm_a_log"][0], np.float32).reshape(64, 1))
    put("dskip", np.broadcast_to(np.asarray(inp["ssm_d"][0], np.float32)[None, :], (128, 64)))
    put("ssmn", _cols(inp["ssm_norm"][0], 32))
    put("ident", np.eye(128, dtype=np.float32))
    put("tri", np.triu(np.ones((128, 128), np.float32)))
    qk = np.stack([np.asarray(inp["attn_q_norm"][0], np.float32), np.asarray(inp["attn_k_norm"][0], np.float32)], 1)
    put("qkn", qk.reshape(6, 128).T)
    NEG = -30000.0
    kk = np.arange(128)[:, None]
    qq = np.arange(128)[None, :]
    put("maskA", np.where(qq >= kk, 0.0, NEG))
    put("maskB", np.where(qq <= kk, 0.0, NEG))
    put("ones", np.ones((128, 128), np.float32))
    return c


def setup_consts(C, cst_d):
    S = C.S
    C.cst = C.alloc_f32(NCST, persist=True)
    C.eps_ap = C.alloc_f32(2, persist=True)[:, 0:1]
    S.add("sp", dma(C.cst, cst_d), writes=["cst"], dma_key="cst")
    S.add("pool", lambda e: e.memset(C.eps_ap, EPS), writes=["eps"])
    o0, o1 = CST["ones"]
    C.ones_f32 = C.cst[:, o0:o1]

    def cs(name):
        a, b = CST[name]
        return C.cst[:, a:b]
    C.ident_bf = C.alloc_bf16(128, persist=True)
    C.ones_bf = C.alloc_bf16(128, persist=True)
    C.maskA4_bf = C.alloc_bf16(512, persist=True)
    C.maskB_bf = C.alloc_bf16(128, persist=True)
    S.add("pool", lambda e: e.tensor_copy(out=C.ident_bf, in_=cs("ident")), reads=["cst"], writes=["cbf_ident"])
    S.add("pool", lambda e: e.tensor_copy(out=C.ones_bf, in_=cs("ones")), reads=["cst"], writes=["cbf_ones"])
    S.add("pool", lambda e: e.tensor_copy(out=C.maskA4_bf.rearrange("p (a b) -> p a b", a=4), in_=cs("maskA").unsqueeze(1).broadcast_to([128, 4, 128])),
          reads=["cst"], writes=["cbf_maskA"])
    S.add("pool", lambda e: e.tensor_copy(out=C.maskB_bf, in_=cs("maskB")), reads=["cst"], writes=["cbf_maskB"])
    C.sel2_bf = C.alloc_bf16(64, persist=True)
    S.add("pool", lambda e: e.tensor_copy(out=C.sel2_bf[0:64, :], in_=cs("ident")[0:64, 0:64]), reads=["cst"], writes=["cbf_sel2"])
    S.add("pool", lambda e: e.tensor_copy(out=C.sel2_bf[64:128, :], in_=cs("ident")[64:128, 64:128]), reads=["cst", "cbf_sel2"], writes=["cbf_sel2"])


NQKV = 9216
SCALE = 128 ** -0.5


def attn_sublayer(C, x_in_d, x_out_d, gain, w_qkv_d, w_ao_d, asp, rope_d, out_key, stop_after=None):
    S = C.S
    cst = C.cst
    P = C.ps

    def cc(name, a=None, b=None):
        o0, o1 = CST[name]
        if a is None:
            return cst[:, o0:o1]
        return cst[:, o0 + a:o0 + b]

    C.new_phase()
    hT = C.alloc_bf16(KC * T).rearrange("p (k n) -> p k n", k=KC)
    rope = C.alloc_f32(2 * T)
    S.add("sp", dma(rope, rope_d), writes=["rope"], dma_key="rope")
    mark = C.off
    rmsnorm_phase(C, x_in_d, gain, hT, "mix1")
    S.barrier()
    C.off = mark
    wb = [[C.alloc_bf16(KC * 512).rearrange("p (k n) -> p k n", k=KC) for _ in range(2)]]
    ub = [C.alloc_f32(1024) for _ in range(2)]
    sqb = [C.alloc_f32(1024) for _ in range(2)]
    rsb = [C.alloc_f32(1024) for _ in range(2)]
    t1b = [C.alloc_f32(1024) for _ in range(2)]
    t2b = [C.alloc_f32(1024) for _ in range(2)]
    qko = [C.alloc_bf16(1024) for _ in range(2)]
    vTb = [C.alloc_bf16(1024) for _ in range(2)]
    vsb = [C.alloc_bf16(1024) for _ in range(2)]
    ones = cc("ones")
    p6b = P[6].bitcast(BF16)
    ctr = [0]

    pending = []

    def flush():
        while pending:
            pending.pop(0)()

    def epi(blk, th, prs, mw):
        flush()
        b = ctr[0] % 2
        ctr[0] += 1
        pp = ("pp", prs[0])
        ps = C.ps2(prs[0])
        cols = slice(th * 1024, (th + 1) * 1024)
        g, j, head = blk // 24, (blk // 8) % 3, blk % 8
        if j < 2:
            gcol = cc("qkn", 2 * g + j, 2 * g + j + 1)
            u, sq, rs, t1, t2 = ub[b], sqb[b], rsb[b], t1b[b], t2b[b]
            S.add("act", lambda e: e.activation(out=u, in_=ps, func=AF.Identity, scale=gcol), reads=[pp, "cst"], writes=[("u", b)])
            S.add("act", lambda e: e.activation(out=sq, in_=ps, func=AF.Square), reads=[pp], writes=[("sq", b)])
            pending.append(lambda: epi_qk_rest(b, g, j, head, th, cols, u, sq, rs, t1, t2))
        else:
            S.add("act", lambda e: e.activation(out=vTb[b], in_=ps, func=AF.Copy), reads=[pp], writes=[("vT", b)])
            pending.append(lambda: epi_v_rest(b, g, head, th))

    def epi_qk_rest(b, g, j, head, th, cols, u, sq, rs, t1, t2):
        if True:
            for t2_ in range(2):
                S.add("pe", lambda e, t2_=t2_: e.matmul(P[6 + t2_][:, :], ones, sq[:, t2_ * 512:(t2_ + 1) * 512], start=True, stop=True),
                      reads=[("sq", b), "cst"], writes=[("pp", 3)])
            S.add("act", lambda e: e.activation(out=rs, in_=C.ps2(3), func=AF.Ln, scale=1.0 / 128, bias=C.eps_ap), reads=[("pp", 3)], writes=[("rs", b)])
            S.add("act", lambda e: e.activation(out=rs, in_=rs, func=AF.Exp, scale=-0.5), reads=[("rs", b)], writes=[("rs", b)])
            S.add("dve", lambda e: e.tensor_tensor(out=t1, in0=u, in1=rope[:, cols], op=ALU.mult), reads=[("u", b), "rope"], writes=[("t1", b)])
            S.add("dve", lambda e: e.tensor_tensor(out=t2[0:64, :], in0=u[64:128, :], in1=rope[64:128, T + th * 1024:T + (th + 1) * 1024], op=ALU.mult),
                  reads=[("u", b), "rope"], writes=[("t2a", b)])
            S.add("dve", lambda e: e.tensor_tensor(out=t2[64:128, :], in0=u[0:64, :], in1=rope[0:64, T + th * 1024:T + (th + 1) * 1024], op=ALU.mult),
                  reads=[("u", b), "rope"], writes=[("t2b", b)])
            S.add("dve", lambda e: e.tensor_tensor(out=t1, in0=t1, in1=t2, op=ALU.add), reads=[("t1", b), ("t2a", b), ("t2b", b)], writes=[("t1", b)])
            S.add("dve", lambda e: e.tensor_tensor(out=qko[b], in0=t1, in1=rs, op=ALU.mult), reads=[("t1", b), ("rs", b)], writes=[("qko", b)])
            bi = (g * 2 + j) * 8 + head
            S.add("sp", dma(asp["qk"][bi * 128:(bi + 1) * 128, cols], qko[b]), reads=[("qko", b)], writes=["qkd"], dma_key=("qko", b))

    def epi_v_rest(b, g, head, th):
        if True:
            for i in range(8):
                S.add("pe", lambda e, i=i: e.transpose(p6b[:, i * 128:(i + 1) * 128], vTb[b][:, i * 128:(i + 1) * 128], C.ident_bf),
                      reads=[("vT", b), "cbf_ident"], writes=[("pp", 3)])
            S.add("dve", lambda e: e.tensor_copy(out=vsb[b], in_=p6b), reads=[("pp", 3)], writes=[("vs", b)])
            gh = g * 8 + head
            r0 = gh * 2048 + th * 1024
            S.add("sp", dma(asp["v"][r0:r0 + 1024, :].rearrange("(k p) d -> p k d", p=128), vsb[b].rearrange("p (k d) -> p k d", k=8)),
                  reads=[("vs", b)], writes=["vd"], dma_key=("vs", b))

    gemm(C, [w_qkv_d], KC, hT, [("hT", i) for i in range(4)], NQKV, 512, wb, epi, pairs=(0, 1, 2))
    flush()
    if stop_after == "A":
        return

    C.new_phase()
    mT = C.alloc_bf16(8 * T).rearrange("p (h n) -> p h n", h=8)
    qb_ = [[C.alloc_bf16(T) for _ in range(3)] for _ in range(2)]
    kb_ = [[C.alloc_bf16(T) for _ in range(3)] for _ in range(2)]
    v0b = [C.alloc_bf16(T).rearrange("p (b d) -> p b d", b=16) for _ in range(2)]
    v1b = [C.alloc_bf16(T).rearrange("p (s r d) -> p s r d", s=4, r=4) for _ in range(2)]
    v2b = [C.alloc_bf16(T).rearrange("p (r d) -> p r d", r=16) for _ in range(2)]
    Pt = [C.alloc_bf16(128) for _ in range(4)]
    rden = C.alloc_f32(1024)
    mA = C.maskA4_bf[:, 0:128]
    mB = C.maskB_bf
    tctr = [0]

    def load_head(h):
        hb = h % 2
        for g in range(3):
            bq = (g * 2 + 0) * 8 + h
            bk = (g * 2 + 1) * 8 + h
            S.add("sp", dma(qb_[hb][g], asp["qk"][bq * 128:(bq + 1) * 128, :]), reads=["qkd"], writes=[("q", hb, g)], dma_key=("q", hb, g))
            S.add("sp", dma(kb_[hb][g], asp["qk"][bk * 128:(bk + 1) * 128, :]), reads=["qkd"], writes=[("k", hb, g)], dma_key=("k", hb, g))
        vv = [asp["v"][(g * 8 + h) * 2048:(g * 8 + h + 1) * 2048, :] for g in range(3)]
        S.add("sp", dma(v0b[hb], vv[0].rearrange("(b p) d -> p b d", p=128)), reads=["vd"], writes=[("v", hb, 0)], dma_key=("v", hb, 0))
        for sb in range(4):
            S.add("sp", dma(v1b[hb][:, sb, :, :], vv[1][sb * 512:(sb + 1) * 512, :].rearrange("(i r) d -> i r d", r=4)), reads=["vd"],
                  writes=[("v", hb, 1, sb)], dma_key=("v", hb, 1, sb))
        S.add("sp", dma(v2b[hb], vv[2].rearrange("(i r) d -> i r d", r=16)), reads=["vd"], writes=[("v", hb, 2)], dma_key=("v", hb, 2))

    def tile_score(k_ap, q_ap, nq, mask_ap, rd):
        t = tctr[0] % 4
        tctr[0] += 1
        sbk = 4 + t
        S.add("pe", lambda e: e.matmul(P[sbk][:, 0:nq], k_ap, q_ap, start=True, stop=False), reads=rd, writes=[("sc", t)])
        S.add("pe", lambda e: e.matmul(P[sbk][:, 0:nq], C.ident_bf, mask_ap, start=False, stop=True), reads=["cbf_ident", "cbf_maskA", "cbf_maskB"], writes=[("sc", t)])
        S.add("act", lambda e: e.activation(out=Pt[t][:, 0:nq], in_=P[sbk][:, 0:nq], func=AF.Exp, scale=SCALE), reads=[("sc", t)], writes=[("pt", t)])
        return t

    def tile_pv(t, v_ap, outs, rd, started, last):
        for (bn, bd, csl, psl) in outs:
            st = bn not in started
            started.add(bn)
            S.add("pe", lambda e, bn=bn, csl=csl, psl=psl, st=st: e.matmul(P[bn][:, csl], v_ap, Pt[t][:, psl], start=st, stop=last),
                  reads=rd + [("pt", t)], writes=[("acc", bn)])
            S.add("pe", lambda e, bd=bd, csl=csl, psl=psl, st=st: e.matmul(P[bd][:, csl], C.ones_bf, Pt[t][:, psl], start=st, stop=last),
                  reads=["cbf_ones", ("pt", t)], writes=[("acc", bd)])

    def do_head(h):
        hb = h % 2
        q0, q1, q2 = qb_[hb]
        k0, k1, k2 = kb_[hb]
        rd0 = [("q", hb, 0), ("k", hb, 0), ("v", hb, 0)]
        rd2 = [("q", hb, 2), ("k", hb, 2), ("v", hb, 2)]
        for qh in range(2):
            started = set()
            tiles = []
            for s_ in range(2):
                sb = 2 * qh + s_
                bn, bd = s_, 2 + s_
                for qq in range(4):
                    qb = 4 * sb + qq
                    csl = slice(qq * 128, (qq + 1) * 128)
                    qa = q0[:, qb * 128:(qb + 1) * 128]
                    if qb > 0:
                        tiles.append((k0[:, (qb - 1) * 128:qb * 128], qa, 128, mB, v0b[hb][:, qb - 1, :], [(bn, bd, csl, slice(0, 128))], rd0))
                    tiles.append((k0[:, qb * 128:(qb + 1) * 128], qa, 128, mA, v0b[hb][:, qb, :], [(bn, bd, csl, slice(0, 128))], rd0))
            for s_ in range(2):
                sb = 2 * qh + s_
                bn, bd = s_, 2 + s_
                for r in range(4):
                    csl = slice(r, 512, 4)
                    qa = q1[:, 512 * sb + r:512 * (sb + 1):4]
                    if sb > 0:
                        rd1p = [("q", hb, 1), ("k", hb, 1), ("v", hb, 1, sb - 1)]
                        tiles.append((k1[:, 512 * (sb - 1) + r:512 * sb:4], qa, 128, mB, v1b[hb][:, sb - 1, r, :], [(bn, bd, csl, slice(0, 128))], rd1p))
                    rd1 = [("q", hb, 1), ("k", hb, 1), ("v", hb, 1, sb)]
                    tiles.append((k1[:, 512 * sb + r:512 * (sb + 1):4], qa, 128, mA, v1b[hb][:, sb, r, :], [(bn, bd, csl, slice(0, 128))], rd1))
            for r in range(16):
                qa = q2[:, 1024 * qh + r:1024 * (qh + 1):16]
                outs = [(s_, 2 + s_, slice(r, 512, 16), slice(32 * s_, 32 * s_ + 32)) for s_ in range(2)]
                tiles.append((k2[:, r:T:16], qa, 64, C.maskA4_bf[:, 64 * qh:64 * qh + 64], v2b[hb][:, r, :], outs, rd2))
            LOOK = 3
            slots = {}
            for ti in range(len(tiles) + LOOK):
                if ti < len(tiles):
                    ka, qa, nq, ma, va, outs, rd = tiles[ti]
                    slots[ti] = tile_score(ka, qa, nq, ma, rd)
                tj = ti - LOOK
                if tj >= 0:
                    ka, qa, nq, ma, va, outs, rd = tiles[tj]
                    tile_pv(slots[tj], va, outs, rd, started, tj == len(tiles) - 1)
            S.add("dve", lambda e: e.reciprocal(out=rden, in_=C.ps_all[:, 1024:2048]), reads=[("acc", 2), ("acc", 3)], writes=["rden"])
            S.add("dve", lambda e, qh=qh, h=h: e.tensor_tensor(out=mT[:, h, qh * 1024:(qh + 1) * 1024], in0=C.ps_all[:, 0:1024], in1=rden, op=ALU.mult),
                  reads=[("acc", 0), ("acc", 1), "rden"], writes=[("mT", h)])

    load_head(0)
    for h in range(8):
        if h + 1 < 8:
            load_head(h + 1)
        do_head(h)
    if stop_after == "B":
        return

    S.barrier()
    wob = [[C.alloc_bf16(8 * 512).rearrange("p (k n) -> p k n", k=8) for _ in range(2)]]
    epi3 = resid_epilogue(C, x_in_d, x_out_d, out_key)
    gemm(C, [w_ao_d], 8, mT, [("mT", h) for h in range(8)], D, 512, wob, epi3)


def make_rope():
    half = 64
    inv_freq = 10000.0 ** (-np.arange(half, dtype=np.float64) / half)
    ang = np.arange(T, dtype=np.float64)[None, :] * inv_freq[:, None]
    cos = np.concatenate([np.cos(ang), np.cos(ang)], 0)
    sin = np.concatenate([np.sin(ang), -np.sin(ang)], 0)
    return np.ascontiguousarray(np.concatenate([cos, sin], 1).astype(np.float32))


ENABLE_MAMBA = True
ENABLE_ATTN = True


def build_program():
    nc = bass.Bass("TRN2", target_bir_lowering=False)

    def din(name, shape):
        return nc.dram_tensor(name, shape, F32, kind="ExternalInput").ap()

    def scratch(name, shape, dt=F32):
        return nc.dram_tensor(name, shape, dt, kind="Internal").ap()

    x_d = din("xT", [D, T])
    cst_d = din("cst", [128, NCST])
    w_in = din("ssm_w_in", [D, 10304])
    w_so = din("ssm_w_out", [4096, D])
    w_qkv = din("attn_w_qkv", [D, 9216])
    w_ao = din("attn_w_out", [1024, D])
    wg = [din("wg%d" % l, [D, DFF]) for l in range(2)]
    wu = [din("wu%d" % l, [D, DFF]) for l in range(2)]
    wd = [din("wd%d" % l, [DFF, D]) for l in range(2)]
    out_d = nc.dram_tensor("outT", [D, T], F32, kind="ExternalOutput").ap()
    x1 = scratch("x1", [D, T])
    x2 = scratch("x2", [D, T])
    x3 = scratch("x3", [D, T])
    act_d = scratch("act_sp", [DFF, T], BF16)
    sp = {"zs": [scratch("zs%d" % i, [1024, T]) for i in range(4)], "xs": [scratch("xs%d" % i, [1024, T]) for i in range(4)],
          "bs": scratch("bs", [1024, T]), "cs": scratch("cs", [1024, T]), "yn": scratch("yn", [4096, T], BF16)}
    asp = {"qk": scratch("qk", [48 * 128, T], BF16), "v": scratch("v", [24 * 2048, 128], BF16)}
    rope_d = din("rope", [128, 2 * T])
    es = ExitStack()
    with es:
        C = Ctx(nc, es)
        S = C.S
        setup_consts(C, cst_d)

        def g(name):
            o0, o1 = CST[name]
            return C.cst[:, o0:o1]

        cur = x_d
        if ENABLE_MAMBA:
            mamba_sublayer(C, cur, x1, g("mixn0"), w_in, w_so, sp, "x1")
            cur = x1
        ffn_sublayer(C, cur, x2, g("ffnn0"), wg[0], wu[0], wd[0], act_d, "x2")
        cur = x2
        if ENABLE_ATTN:
            attn_sublayer(C, cur, x3, g("mixn1"), w_qkv, w_ao, asp, rope_d, "x3")
            cur = x3
        ffn_sublayer(C, cur, out_d, g("ffnn1"), wg[1], wu[1], wd[1], act_d, "out")
        S.add("sp", lambda e: e.nop(), reads=[("out", 0), ("out", 1)])
        S.emit(nc, es)
    return nc


def kernel(**inputs):
    inp = {k: np.asarray(v) for k, v in inputs.items()}
    nc = build_program()
    cst = make_cst(inp)
    shared = {
        "cst": cst,
        "rope": make_rope(),
        "ssm_w_in": np.ascontiguousarray(inp["ssm_w_in"][0], dtype=np.float32),
        "ssm_w_out": np.ascontiguousarray(inp["ssm_w_out"][0], dtype=np.float32),
        "attn_w_qkv": np.ascontiguousarray(inp["attn_w_qkv"][0], dtype=np.float32),
        "attn_w_out": np.ascontiguousarray(inp["attn_w_out"][0], dtype=np.float32),
    }
    for l in range(2):
        shared["wg%d" % l] = np.ascontiguousarray(inp["ffn_w_gate"][l], dtype=np.float32)
        shared["wu%d" % l] = np.ascontiguousarray(inp["ffn_w_up"][l], dtype=np.float32)
        shared["wd%d" % l] = np.ascontiguousarray(inp["ffn_w_down"][l], dtype=np.float32)
    x = inp["x"].astype(np.float32)
    in_maps = []
    for b in range(8):
        m = dict(shared)
        m["xT"] = np.ascontiguousarray(x[b].T)
        in_maps.append(m)
    res = run_bass_kernel_spmd(nc, in_maps, core_ids=list(range(8)))
    out = np.stack([np.ascontiguousarray(r["outT"].T) for r in res.results], 0)
    return out.astype(np.float32)
```
